# Optimizing a Trainium2 kernel written in Bass

```python
import math
import jax, jax.numpy as jnp
from jax import lax
import numpy as np

D_MODEL = 2048
BATCH = 2
SEQ = 16384
DEPTH = 4
DEC_BATCH = 8
DEC_SEQ = 4096
PAST_LEN = 128

HEAD_DIM = 128
N_HEADS = 8
N_KV = 2
GQA_G = N_HEADS // N_KV
ATTN_W = N_HEADS * HEAD_DIM
KV_W = N_KV * HEAD_DIM
C_CONV = D_MODEL - ATTN_W
MIX_W = ATTN_W + C_CONV
IN_W = ATTN_W + 2 * KV_W + 2 * C_CONV
CONV_K = 31
WINDOW = 128
BLK = 128
ROT_DIM = HEAD_DIM // 4
ROPE_THETA = 500000.0
D_FF = ((int(math.ceil(8 * D_MODEL / 3)) + 255) // 256) * 256
PLE_DIM = 256
EPS = 1e-6

kernel_name = "hymba_conformer_swa_encoder"


def rmsnorm(x, g):
    xf = x.astype(jnp.float32)
    y = xf * lax.rsqrt(jnp.mean(xf * xf, axis=-1, keepdims=True) + EPS)
    return (y * g.astype(jnp.float32)).astype(x.dtype)


def layernorm(x, g, b):
    xf = x.astype(jnp.float32)
    mu = jnp.mean(xf, axis=-1, keepdims=True)
    var = jnp.mean(jnp.square(xf - mu), axis=-1, keepdims=True)
    y = (xf - mu) * lax.rsqrt(var + EPS)
    return (y * g.astype(jnp.float32) + b.astype(jnp.float32)).astype(x.dtype)


def partial_rope(x, pos):
    half = ROT_DIM // 2
    inv_freq = jnp.exp(-math.log(ROPE_THETA) * jnp.arange(0, ROT_DIM, 2, dtype=jnp.float32) / ROT_DIM)
    ang = pos[:, None] * inv_freq[None, :]
    cos = jnp.cos(ang)[:, None, :]
    sin = jnp.sin(ang)[:, None, :]
    xr = x[..., :ROT_DIM].astype(jnp.float32)
    x1, x2 = xr[..., :half], xr[..., half:]
    rot = jnp.concatenate([x1 * cos - x2 * sin, x2 * cos + x1 * sin], axis=-1).astype(x.dtype)
    return jnp.concatenate([rot, x[..., ROT_DIM:]], axis=-1)


def banded_sink_attention(q, k, v, sink):
    B, S = q.shape[0], q.shape[1]
    nb = S // BLK
    qb = (q * (HEAD_DIM ** -0.5)).reshape(B, nb, BLK, N_KV, GQA_G, HEAD_DIM)
    pad = ((0, 0), (WINDOW, WINDOW), (0, 0), (0, 0))
    kp = jnp.pad(k, pad).reshape(B, nb + 2, BLK, N_KV, HEAD_DIM)
    vp = jnp.pad(v, pad).reshape(B, nb + 2, BLK, N_KV, HEAD_DIM)
    kb = jnp.concatenate([kp[:, :-2], kp[:, 1:-1], kp[:, 2:]], axis=2)
    vb = jnp.concatenate([vp[:, :-2], vp[:, 1:-1], vp[:, 2:]], axis=2)
    s = jnp.einsum('bnqhgd,bnkhd->bnhgqk', qb, kb, preferred_element_type=jnp.float32)
    n_idx = jnp.arange(nb)[:, None, None]
    t_idx = jnp.arange(BLK)[None, :, None]
    s_idx = jnp.arange(3 * BLK)[None, None, :]
    kpos = (n_idx - 1) * BLK + s_idx
    rel = s_idx - BLK - t_idx
    valid = (jnp.abs(rel) <= WINDOW) & (kpos >= 0) & (kpos < S)
    s = jnp.where(valid[None, :, None, None], s, -jnp.inf)
    sk = sink.astype(jnp.float32).reshape(1, 1, N_KV, GQA_G, 1)
    m = jnp.maximum(jnp.max(s, axis=-1), sk)
    e = jnp.exp(s - m[..., None])
    denom = jnp.sum(e, axis=-1) + jnp.exp(sk - m)
    probs = (e / denom[..., None]).astype(v.dtype)
    out = jnp.einsum('bnhgqk,bnkhd->bnqhgd', probs, vb, preferred_element_type=jnp.float32)
    return out.astype(q.dtype).reshape(B, S, ATTN_W)


def conformer_conv(u, conv_w, conv_b, ln_g, ln_b):
    a, b = u[..., :C_CONV], u[..., C_CONV:]
    g = a * jax.nn.sigmoid(b)
    y = lax.conv_general_dilated(
        g, conv_w[:, None, :].astype(g.dtype), window_strides=(1,),
        padding=[((CONV_K - 1) // 2, (CONV_K - 1) // 2)],
        dimension_numbers=('NWC', 'WIO', 'NWC'), feature_group_count=C_CONV)
    y = y + conv_b
    return jax.nn.silu(layernorm(y, ln_g, ln_b))


def encoder_layer(h, p_l, g_mix, w_in, sink, conv_w, conv_b, conv_ln_g, conv_ln_b,
                  g_attn_out, g_conv_out, w_out, g_ffn, w_gate, w_up, w_down,
                  g_ple, w_ple_gate, w_ple):
    B, S, _ = h.shape
    a = rmsnorm(h, g_mix)
    z = a @ w_in
    q = z[..., :ATTN_W].reshape(B, S, N_HEADS, HEAD_DIM)
    k = z[..., ATTN_W:ATTN_W + KV_W].reshape(B, S, N_KV, HEAD_DIM)
    v = z[..., ATTN_W + KV_W:ATTN_W + 2 * KV_W].reshape(B, S, N_KV, HEAD_DIM)
    u = z[..., ATTN_W + 2 * KV_W:]
    pos = jnp.arange(S, dtype=jnp.float32)
    attn = banded_sink_attention(partial_rope(q, pos), partial_rope(k, pos), v, sink)
    conv = conformer_conv(u, conv_w, conv_b, conv_ln_g, conv_ln_b)
    mixed = jnp.concatenate([rmsnorm(attn, g_attn_out), rmsnorm(conv, g_conv_out)], axis=-1)
    h = h + mixed @ w_out
    f = rmsnorm(h, g_ffn)
    h = h + (jax.nn.silu(f @ w_gate) * (f @ w_up)) @ w_down
    gate = jax.nn.sigmoid(rmsnorm(h, g_ple) @ w_ple_gate)
    return h + (p_l @ w_ple) * gate


def setup_inputs(seed: int = 0) -> dict:
    key = jax.random.key(seed)
    ks = jax.random.split(key, 24)
    f32 = jnp.float32

    def nrm(k, shape, scale):
        return jax.random.normal(k, shape, f32) * scale

    def gain(k, shape):
        return 1.0 + 0.05 * jax.random.normal(k, shape, f32)

    return {
        "x_prompt": nrm(ks[0], (BATCH, SEQ, D_MODEL), 1.0),
        "x_sample": nrm(ks[1], (DEC_BATCH, DEC_SEQ, D_MODEL), 1.0),
        "p_prompt": nrm(ks[2], (DEPTH, BATCH, SEQ, PLE_DIM), 1.0),
        "p_sample": nrm(ks[3], (DEPTH, DEC_BATCH, DEC_SEQ, PLE_DIM), 1.0),
        "g_mix": gain(ks[4], (DEPTH, D_MODEL)),
        "w_in": nrm(ks[5], (DEPTH, D_MODEL, IN_W), D_MODEL ** -0.5),
        "sink": nrm(ks[6], (DEPTH, N_HEADS), 0.5),
        "conv_w": nrm(ks[7], (DEPTH, CONV_K, C_CONV), CONV_K ** -0.5),
        "conv_b": nrm(ks[8], (DEPTH, C_CONV), 0.02),
        "conv_ln_g": gain(ks[9], (DEPTH, C_CONV)),
        "conv_ln_b": nrm(ks[10], (DEPTH, C_CONV), 0.02),
        "g_attn_out": gain(ks[11], (DEPTH, ATTN_W)),
        "g_conv_out": gain(ks[12], (DEPTH, C_CONV)),
        "w_out": nrm(ks[13], (DEPTH, MIX_W, D_MODEL), MIX_W ** -0.5),
        "g_ffn": gain(ks[14], (DEPTH, D_MODEL)),
        "w_gate": nrm(ks[15], (DEPTH, D_MODEL, D_FF), D_MODEL ** -0.5),
        "w_up": nrm(ks[16], (DEPTH, D_MODEL, D_FF), D_MODEL ** -0.5),
        "w_down": nrm(ks[17], (DEPTH, D_FF, D_MODEL), D_FF ** -0.5),
        "g_ple": gain(ks[18], (DEPTH, D_MODEL)),
        "w_ple_gate": nrm(ks[19], (DEPTH, D_MODEL, D_MODEL), D_MODEL ** -0.5),
        "w_ple": nrm(ks[20], (DEPTH, PLE_DIM, D_MODEL), PLE_DIM ** -0.5),
        "g_final": gain(ks[21], (D_MODEL,)),
    }


def reference(x_prompt, x_sample, p_prompt, p_sample, g_mix, w_in, sink, conv_w, conv_b,
              conv_ln_g, conv_ln_b, g_attn_out, g_conv_out, w_out, g_ffn, w_gate, w_up,
              w_down, g_ple, w_ple_gate, w_ple, g_final):
    def trunk(x, p):
        h = x
        for i in range(DEPTH):
            h = encoder_layer(h, p[i], g_mix[i], w_in[i], sink[i], conv_w[i], conv_b[i],
                              conv_ln_g[i], conv_ln_b[i], g_attn_out[i], g_conv_out[i],
                              w_out[i], g_ffn[i], w_gate[i], w_up[i], w_down[i],
                              g_ple[i], w_ple_gate[i], w_ple[i])
        return rmsnorm(h, g_final)

    y_prompt = trunk(x_prompt, p_prompt)
    y_sample = trunk(x_sample, p_sample)
    return (y_prompt, y_sample)
```

```python
import math
import numpy as np
import concourse.bass as bass
import concourse.mybir as mybir
from concourse.bass_utils import run_bass_kernel_spmd

F32 = mybir.dt.float32
BF16 = mybir.dt.bfloat16
AF = mybir.ActivationFunctionType
ALU = mybir.AluOpType
AX = mybir.AxisListType

D = 2048
NCH = 16
T = 512
DFF = 5632
NFF = 44
NFH = 22
PLE = 256
CONVK = 31
EPS = 1e-6
ROPE_THETA = 500000.0
N_CORES = 8
NT_FULL = 17
NW_SLOTS = 5
SCALE = 128 ** -0.5

V_GMIX, V_GFFN, V_GPLE, V_CB, V_LNG, V_LNB, V_GCO, V_GAO, V_CW, V_SINK = 0, 16, 32, 48, 56, 64, 72, 80, 88, 336
V_LAYER = 344


class Tracker:
    def __init__(self, nc, sems):
        self.nc = nc
        self.engs = {"pe": nc.tensor, "act": nc.scalar, "dve": nc.vector, "pool": nc.gpsimd, "sp": nc.sync}
        self.ops = {e: [] for e in self.engs}
        self.cnt = {e: 0 for e in ("pe", "act", "dve", "pool")}
        self.sem = sems
        self.res = {}
        self.known = {e: {} for e in self.engs}
        self.dq = {"sp": [f"sp{i}" for i in range(8)], "pool": [f"pl{i}" for i in range(8)]}
        self.dqi = {"sp": 0, "pool": 0}
        self.dcnt = {}
        for q in self.dq.values():
            for s in q:
                self.dcnt[s] = 0
        self.untracked = set()

    def _deps(self, reads, writes):
        deps = {}

        def add(tok):
            if tok is None:
                return
            s, v = tok
            if deps.get(s, 0) < v:
                deps[s] = v
        for k in reads:
            if k in self.untracked:
                continue
            r = self.res.get(k)
            if r:
                add(r[0])
        for k in writes:
            r = self.res.get(k)
            if r:
                add(r[0])
                for t in r[1]:
                    add(t)
        return deps

    def _commit(self, eng, deps, fn, tok, reads, writes):
        kn = self.known[eng]
        waits = []
        for s, v in deps.items():
            if kn.get(s, 0) < v:
                waits.append((s, v))
                kn[s] = v
        self.ops[eng].append((waits, fn, tok))
        for k in reads:
            if k in self.untracked:
                continue
            r = self.res.get(k)
            if r is None:
                self.res[k] = [None, [tok]]
            else:
                r[1].append(tok)
        for k in writes:
            self.res[k] = [tok, []]

    def op(self, eng, fn, reads=(), writes=()):
        deps = self._deps(reads, writes)
        self.cnt[eng] += 1
        tok = (eng, self.cnt[eng])
        self._commit(eng, deps, fn, tok, reads, writes)
        return tok

    def dma(self, q, fn, reads=(), writes=()):
        deps = self._deps(reads, writes)
        ring = self.dq[q]
        s = ring[self.dqi[q] % len(ring)]
        self.dqi[q] += 1
        prev = self.dcnt[s]
        if prev > 0 and deps.get(s, 0) < prev:
            deps[s] = prev
        self.dcnt[s] = prev + 16
        tok = (s, prev + 16)
        self._commit(q, deps, fn, tok, reads, writes)
        return tok

    def sync_all(self):
        allt = {}
        for e, c in self.cnt.items():
            if c:
                allt[e] = c
        for s, c in self.dcnt.items():
            if c:
                allt[s] = c
        for e in self.engs:
            kn = self.known[e]
            waits = []
            for s, v in allt.items():
                if kn.get(s, 0) < v:
                    waits.append((s, v))
                    kn[s] = v
            if waits:
                self.ops[e].append((waits, None, None))

    def emit(self, block):
        tr = self

        def run(ename, e):
            for waits, fn, tok in tr.ops[ename]:
                for s, v in waits:
                    e.wait_ge(tr.sem[s], v)
                if fn is None:
                    continue
                ins = fn(e)
                s, v = tok
                ins.then_inc(tr.sem[s], 16 if s not in tr.cnt else 1)

        @block.tensor
        def _(e):
            run("pe", e)

        @block.scalar
        def _(e):
            run("act", e)

        @block.vector
        def _(e):
            run("dve", e)

        @block.gpsimd
        def _(e):
            run("pool", e)

        @block.sync
        def _(e):
            run("sp", e)


def build_program(NT=NT_FULL, NL=4, dbg=False, stage=99):
    nc = bass.Bass("TRN2", target_bir_lowering=False)
    NTOK = NT * T

    def din(name, shape, dt=F32):
        return nc.dram_tensor(name, list(shape), dt, kind="ExternalInput").ap()

    def dscr(name, shape, dt):
        return nc.dram_tensor(name, list(shape), dt, kind="Internal").ap()

    x_d = din("x", [NTOK, D])
    p_d = din("p", [NL, NTOK, PLE])
    w_in_d = din("w_in", [NL, D, 3584])
    w_out_d = din("w_out", [NL, D, D])
    w_gate_d = din("w_gate", [NL, D, DFF])
    w_up_d = din("w_up", [NL, D, DFF])
    w_down_d = din("w_down", [NL, DFF, D])
    w_pg_d = din("w_ple_gate", [NL, D, D])
    w_pl_d = din("w_ple", [NL, PLE, D])
    NV = NL * V_LAYER + 16 + 2 * NT
    vecs_d = din("vecs", [128, NV])
    rope_d = din("rope", [NT, 2, 32, T])
    cst_d = din("consts", [128, 5 * 128])
    y_d = nc.dram_tensor("y", [NTOK, D], F32, kind="ExternalOutput").ap()

    hs = [dscr(f"hs{i}", [128, NCH, NTOK], F32) for i in range(2)]
    qs = dscr("qs", [128, 8, NTOK], BF16)
    ks = dscr("ks", [128, 2, NTOK + 256], BF16)
    vs = dscr("vs", [NTOK + 256, 2, 128], BF16)
    gs = dscr("gs", [128, 8, NTOK + 32], BF16)
    wb_in = dscr("wb_in", [NL, 26, 128, 16, 128], BF16)
    wb_v = dscr("wb_v", [NL, 2, 128, 8, 256], BF16)
    wb_out = dscr("wb_out", [NL, 16, 128, 16, 128], BF16)
    wb_gate = dscr("wb_gate", [NL, NFF, 128, 16, 128], BF16)
    wb_up = dscr("wb_up", [NL, NFF, 128, 16, 128], BF16)
    wb_down = dscr("wb_down", [NL, 16, 4, 128, 11, 128], BF16)
    wb_pg = dscr("wb_pg", [NL, 16, 128, 16, 128], BF16)
    wb_pl = dscr("wb_pl", [NL, 2, 128, D], BF16)

    if dbg:
        dbg_mixed = dscr("dbg_mixed", [128, NCH, T], BF16)
        dbg_h1 = dscr("dbg_h1", [128, NCH, T], F32)
        dbg_h2 = dscr("dbg_h2", [128, NCH, T], F32)
    sem_names = ["pe", "act", "dve", "pool"] + [f"sp{i}" for i in range(8)] + [f"pl{i}" for i in range(8)]

    import contextlib
    with contextlib.ExitStack() as es:
        def sb(name, shape, dt):
            return es.enter_context(nc.sbuf_tensor("sb_" + name, list(shape), dt))

        h = sb("h", [128, NCH, T], F32)
        xa = sb("xa", [128, NCH, T], BF16)
        xb = sb("xb", [128, NCH, T], BF16)
        qb = sb("qb", [128, 8, T], BF16)
        kb = sb("kb", [128, 2, 768], BF16)
        vb = sb("vb", [128, 6, 2, 128], BF16)
        gw = sb("gw", [128, 8, 544], BF16)
        acc = sb("acc", [128, 8, T], F32)
        ctmp = sb("ctmp", [128, 8, T], BF16)
        act = sb("act", [128, NFH, T], BF16)
        ptb = sb("ptb", [128, 4, 3, 128], BF16)
        attn = sb("attn", [128, 8, 128], F32)
        attn_n = sb("attn_n", [128, 8, 128], BF16)
        junk = sb("junk", [128, 1024], BF16)
        small = sb("small", [128, 64], F32)
        ptok = sb("ptok", [128, 4, PLE], F32)
        pT = sb("pT", [128, 2, T], BF16)
        rstd = sb("rstd", [128, 2, T], F32)
        tmpf = sb("tmpf", [128, 2, T], F32)
        ropet = sb("ropet", [32, 2, T], F32)
        zr = sb("zr", [32, T], BF16)
        zc = sb("zc", [32, 2, T], F32)
        wring = sb("wring", [128, NW_SLOTS, 2048], BF16)
        wple = sb("wple", [128, 2, D], BF16)
        vecs = sb("vecs", [128, NV], F32)
        cstf = sb("cstf", [128, 5 * 128], F32)
        cstb = sb("cstb", [128, 5 * 128], BF16)
        onesd = sb("onesd", [128, 128], BF16)
        onesc = sb("onesc", [128, 128], BF16)
        m3 = sb("m3", [128, 3, 384], BF16)
        esink = sb("esink", [128, 8], F32)
        zpad = sb("zpad", [128, 2, 128], BF16)
        zg = sb("zg", [128, 8, 16], BF16)
        epsb = sb("epsb", [128, 1], F32)
        ps = es.enter_context(nc.psum_tensor("ps", [128, 8, T], F32))
        sems = {n: es.enter_context(nc.semaphore(n)) for n in sem_names}
        print("sbuf bytes remaining", nc.sbuf_bytes_remaining, flush=True)
        block = es.enter_context(nc.Block())

        tr = Tracker(nc, sems)
        ident_f = cstf[:, 0:128]
        ident_b = cstb[:, 0:128]
        tri_ge = cstb[:, 128:256]
        tri_le = cstb[:, 256:384]
        pswap = cstb[0:32, 384:416]
        ones_col = cstb[:, 512:513]

        tr.dma("pool", lambda e: e.dma_start(out=vecs[:, :], in_=vecs_d[:, :]), writes=["vecs"])
        tr.dma("pool", lambda e: e.dma_start(out=cstf[:, :], in_=cst_d[:, :]), writes=["cstf"])
        tr.op("dve", lambda e: e.tensor_copy(out=cstb[:, :], in_=cstf[:, :]), reads=["cstf"], writes=["cstb"])
        for v_ in range(3):
            tr.op("dve", lambda e, v_=v_: e.tensor_copy(out=m3[:, v_, 0:128], in_=cstb[:, 128:256]), reads=["cstb"], writes=[("m3", v_)])
            tr.op("dve", lambda e, v_=v_: e.tensor_copy(out=m3[:, v_, 128:256], in_=cstb[:, 512:640]), reads=["cstb"], writes=[("m3", v_)])
            tr.op("dve", lambda e, v_=v_: e.tensor_copy(out=m3[:, v_, 256:384], in_=cstb[:, 256:384]), reads=["cstb"], writes=[("m3", v_)])
        tr.op("dve", lambda e: e.memset(onesd[:, :], 1.0 / D), writes=["onesd"])
        tr.op("dve", lambda e: e.memset(onesc[:, :], 1.0 / 1024), writes=["onesc"])
        tr.op("dve", lambda e: e.memset(zpad[:, :, :], 0.0), writes=["zpad"])
        tr.op("dve", lambda e: e.memset(zg[:, :, :], 0.0), writes=["zg"])
        tr.op("dve", lambda e: e.memset(epsb[:, :], EPS), writes=["epsb"])
        tr.dma("pool", lambda e: e.dma_start(out=ks[:, :, 0:128], in_=zpad[:, :, :]), reads=["zpad"], writes=[("ks", -1)])
        tr.dma("pool", lambda e: e.dma_start(out=ks[:, :, NTOK + 128:NTOK + 256], in_=zpad[:, :, :]), reads=["zpad"], writes=[("ks", NT)])
        tr.dma("pool", lambda e: e.dma_start(out=vs[0:128, :, :], in_=zpad[:, :, :]), reads=["zpad"], writes=[("vs", -1)])
        tr.dma("pool", lambda e: e.dma_start(out=vs[NTOK + 128:NTOK + 256, :, :], in_=zpad[:, :, :]), reads=["zpad"], writes=[("vs", NT)])
        tr.dma("pool", lambda e: e.dma_start(out=gs[:, :, 0:16], in_=zg[:, :, :]), reads=["zg"], writes=[("gs", -1)])
        tr.dma("pool", lambda e: e.dma_start(out=gs[:, :, NTOK + 16:NTOK + 32], in_=zg[:, :, :]), reads=["zg"], writes=[("gs", NT)])

        tr.sync_all()
        tr.untracked.update(["vecs", "cstb", "cstf", "onesd", "onesc"])
        def cast_list(l):
            lst = []

            def fm(dst, src_cols, key):
                lst.append((key, lambda e, dst=dst, src=src_cols: e.dma_start(
                    out=dst, in_=src.rearrange("(kc p) m -> p kc m", p=128))))
            for j in range(8):
                fm(wb_in[l, j], w_in_d[l, :, j * 128:(j + 1) * 128], ("wb_in", l, j))
            for j in range(2):
                fm(wb_in[l, 8 + j], w_in_d[l, :, 1024 + j * 128:1024 + (j + 1) * 128], ("wb_in", l, 8 + j))
            for j in range(16):
                fm(wb_in[l, 10 + j], w_in_d[l, :, 1536 + j * 128:1536 + (j + 1) * 128], ("wb_in", l, 10 + j))
            for pc in range(2):
                lst.append((("wb_v", l, pc), lambda e, pc=pc: e.dma_start(
                    out=wb_v[l, pc], in_=w_in_d[l, pc * 1024:(pc + 1) * 1024, 1280:1536].rearrange("(kc p) m -> p kc m", p=128))))
            for j in range(16):
                fm(wb_out[l, j], w_out_d[l, :, j * 128:(j + 1) * 128], ("wb_out", l, j))
            for j in range(NFF):
                fm(wb_gate[l, j], w_gate_d[l, :, j * 128:(j + 1) * 128], ("wb_gate", l, j))
                fm(wb_up[l, j], w_up_d[l, :, j * 128:(j + 1) * 128], ("wb_up", l, j))
            for j in range(16):
                for pc in range(4):
                    lst.append((("wb_down", l, j, pc), lambda e, j=j, pc=pc: e.dma_start(
                        out=wb_down[l, j, pc],
                        in_=w_down_d[l, pc * 1408:(pc + 1) * 1408, j * 128:(j + 1) * 128].rearrange("(kc p) m -> p kc m", p=128))))
            for j in range(16):
                fm(wb_pg[l, j], w_pg_d[l, :, j * 128:(j + 1) * 128], ("wb_pg", l, j))
            for kc in range(2):
                lst.append((("wb_pl", l, kc), lambda e, kc=kc: e.dma_start(
                    out=wb_pl[l, kc], in_=w_pl_d[l, kc * 128:(kc + 1) * 128, :])))
            return lst

        pending_casts = []

        def issue_casts(n):
            for _ in range(min(n, len(pending_casts))):
                key, fn = pending_casts.pop(0)
                tr.dma("pool", fn, writes=[key])

        if stage >= 1:
            pending_casts.extend(cast_list(0))
            issue_casts(28)

        wstate = {"n": 0}

        def wload(src_ap, key, shape3=None):
            slot = wstate["n"] % NW_SLOTS
            wstate["n"] += 1
            nel = 1
            for s in src_ap.shape[1:]:
                nel *= s
            dst = wring[:, slot, 0:nel]
            if len(src_ap.shape) == 3:
                dstv = dst.rearrange("p (a b) -> p a b", b=src_ap.shape[2])
            else:
                dstv = dst
            rk = ("wr", slot)
            tr.dma("sp", lambda e: e.dma_start(out=dstv, in_=src_ap), reads=[key], writes=[rk])
            return dstv, rk

        def mm_group(bank_ap, bank_key, parts, reads, extra_writes=(), part_reads=None):
            if part_reads is not None:
                n = len(parts)
                tok = None
                for i, (l_, r_) in enumerate(parts):
                    def fn1(e, i=i, l_=l_, r_=r_):
                        return e.matmul(bank_ap, lhsT=l_, rhs=r_, start=(i == 0), stop=(i == n - 1))
                    wr = [bank_key] + list(extra_writes) if (i == 0 or i == n - 1) else []
                    tok = tr.op("pe", fn1, reads=list(reads) + list(part_reads[i]), writes=wr)
                return tok
            def fn(e):
                ins = None
                n = len(parts)
                for i, (l_, r_) in enumerate(parts):
                    ins = e.matmul(bank_ap, lhsT=l_, rhs=r_, start=(i == 0), stop=(i == n - 1))
                return ins
            return tr.op("pe", fn, reads=reads, writes=[bank_key] + list(extra_writes))

        def vcol(l, base, c):
            o = l * V_LAYER + base + c
            return vecs[:, o:o + 1]

        def norm_stats(src_key_fn, src_ap_fn, sqbuf, sq_key_fn, nchunks, ones, rs_slot, pbank=4):
            for c in range(nchunks):
                rk_ = src_key_fn(c)
                tr.op("act", lambda e, c=c: e.activation(out=sqbuf[:, c, :], in_=src_ap_fn(c), func=AF.Square),
                      reads=(rk_ if isinstance(rk_, list) else [rk_]), writes=[sq_key_fn(c)])
            mm_group(ps[:, pbank, :], ("ps", pbank), [(ones[:, :], sqbuf[:, c, :]) for c in range(nchunks)],
                     reads=[sq_key_fn(c) for c in range(nchunks)])
            if stage < 3.05:
                return
            tr.op("act", lambda e: e.activation(out=rstd[:, rs_slot, :], in_=ps[:, pbank, :], func=AF.Sqrt, bias=epsb[:, 0:1], scale=1.0),
                  reads=[("ps", pbank)], writes=[("rstd", rs_slot)])
            tr.op("dve", lambda e: e.reciprocal(out=rstd[:, rs_slot, :], in_=rstd[:, rs_slot, :]),
                  reads=[("rstd", rs_slot)], writes=[("rstd", rs_slot)])

        for t in range(NT if stage >= 2 else 0):
            for b in range(4):
                half = b % 2
                stg = acc[:, half * 4:(half + 1) * 4, :].rearrange("p a b -> p (a b)")
                skey = ("accs", half)
                tok0 = t * T + b * 128
                tr.dma("pool", lambda e, stg=stg, tok0=tok0: e.dma_start(out=stg, in_=x_d[tok0:tok0 + 128, :]), writes=[skey])
                for c in range(NCH):
                    bank = 5 + (c % 2)

                    def fn(e, c=c, bank=bank, stg=stg):
                        return e.transpose(ps[:, bank, 0:128], stg[:, c * 128:(c + 1) * 128], ident_f)
                    tr.op("pe", fn, reads=[skey], writes=[("ps", bank)])
                    eng = "dve" if c % 2 == 0 else "act"
                    if eng == "dve":
                        tr.op("dve", lambda e, c=c, bank=bank, b=b: e.tensor_copy(out=h[:, c, b * 128:(b + 1) * 128], in_=ps[:, bank, 0:128]),
                              reads=[("ps", bank)], writes=[("h", c)])
                    else:
                        tr.op("act", lambda e, c=c, bank=bank, b=b: e.activation(out=h[:, c, b * 128:(b + 1) * 128], in_=ps[:, bank, 0:128], func=AF.Copy),
                              reads=[("ps", bank)], writes=[("h", c)])
            tr.dma("pool", lambda e, t=t: e.dma_start(out=hs[0][:, :, t * T:(t + 1) * T], in_=h[:, :, :]),
                   reads=[("h", c) for c in range(NCH)], writes=[("hs", 0, t)])
            issue_casts((250 + NT - 1) // NT)
        issue_casts(10 ** 9)

        def do_layer(l):
            hin, hout = hs[l % 2], hs[(l + 1) % 2]
            hin_i, hout_i = l % 2, (l + 1) % 2
            if l + 1 < NL:
                pending_casts.extend(cast_list(l + 1))
            casts_per_tile = (len(pending_casts) + NT - 1) // NT if l + 1 < NL else 0
            tr.op("act", lambda e, l=l: e.activation(out=esink[:, :], in_=vecs[:, l * V_LAYER + V_SINK:l * V_LAYER + V_SINK + 8], func=AF.Exp),
                  reads=["vecs"], writes=["esink"])

            tr.dma("sp", lambda e, l=l: e.dma_start(out=wple[:, :, :], in_=wb_pl[l].rearrange("k p n -> p k n")),
                   reads=[("wb_pl", l, 0), ("wb_pl", l, 1)], writes=["wple"])
            actf = act[:, :, :].rearrange("p a b -> p (a b)").bitcast(F32)

            def h1_ap(t, c):
                if t % 2 == 0:
                    return h[:, c, :]
                return acc[:, c, :] if c < 8 else actf[:, (c - 8) * T:(c - 7) * T]

            def h1_keys(t, c):
                if t % 2 == 0:
                    return [("h", c)]
                return [("acc", c), ("accs", c // 4)] if c < 8 else [("act", 2 * (c - 8)), ("act", 2 * (c - 8) + 1)]

            def h1_load(t):
                if t % 2 == 0:
                    tr.dma("pool", lambda e: e.dma_start(out=h[:, :, :], in_=hin[:, :, t * T:(t + 1) * T]),
                           reads=[("hs", hin_i, t)], writes=[("h", c) for c in range(NCH)])
                else:
                    tr.dma("pool", lambda e: e.dma_start(out=acc[:, :, :], in_=hin[:, 0:8, t * T:(t + 1) * T]),
                           reads=[("hs", hin_i, t)], writes=[k_ for c in range(8) for k_ in h1_keys(t, c)])
                    tr.dma("pool", lambda e: e.dma_start(out=actf[:, 0:8 * T].rearrange("p (a b) -> p a b", b=T), in_=hin[:, 8:16, t * T:(t + 1) * T]),
                           reads=[("hs", hin_i, t)], writes=[k_ for c in range(8, 16) for k_ in h1_keys(t, c)])

            def pass1_tile(t):
                if t == 0:
                    h1_load(0)
                tr.dma("pool", lambda e, t=t: e.dma_start(out=ropet[:, :, :], in_=rope_d[t].rearrange("a p n -> p a n")), writes=["rope"])
                if t + 1 < NT:
                    h1_load(t + 1)
                if stage < 3.02:
                    return
                norm_stats(lambda c: h1_keys(t, c), lambda c: h1_ap(t, c), xa, lambda c: ("xa", c), NCH, onesd, 0)
                if stage < 3.08:
                    return
                for c in range(NCH):
                    tr.op("dve", lambda e, c=c: e.scalar_tensor_tensor(out=xa[:, c, :], in0=h1_ap(t, c), scalar=vcol(l, V_GMIX, c), in1=rstd[:, 0, :],
                                                                        op0=ALU.mult, op1=ALU.mult),
                          reads=h1_keys(t, c) + [("rstd", 0), "vecs"], writes=[("xa", c)])
                xa_keys = [("xa", c) for c in range(NCH)]
                if stage < 3.2:
                    return
                for j in range(10):
                    wv, rk = wload(wb_in[l, j], ("wb_in", l, j))
                    bank = j % 4
                    if j == 0:
                        mm_group(ps[:, bank, :], ("ps", bank), [(wv[:, kc, :], xa[:, kc, :]) for kc in range(NCH)], reads=[rk],
                                 part_reads=[[("xa", kc)] for kc in range(NCH)])
                    else:
                        mm_group(ps[:, bank, :], ("ps", bank), [(wv[:, kc, :], xa[:, kc, :]) for kc in range(NCH)], reads=[rk] + xa_keys)
                    if j < 8:
                        dst, dkey = qb[:, j, :], ("q", j)
                    else:
                        dst, dkey = kb[:, j - 8, 0:T], ("k", j - 8)
                    tr.op("act", lambda e, dst=dst, bank=bank: e.activation(out=dst, in_=ps[:, bank, :], func=AF.Copy),
                          reads=[("ps", bank)], writes=[(dkey, "hi"), (dkey, "lo")])
                    tr.op("dve", lambda e, bank=bank: e.tensor_tensor(out=zc[:, 0, :], in0=ps[0:32, bank, :], in1=ropet[:, 0, :], op=ALU.mult),
                          reads=[("ps", bank), "rope", (dkey, "hi")], writes=[("zc", 0)])
                    mm_group(ps[0:32, 5, :], ("ps", 5), [(pswap, dst[0:32, :])], reads=[(dkey, "lo")])
                    tr.op("dve", lambda e: e.tensor_tensor(out=zc[:, 1, :], in0=ps[0:32, 5, :], in1=ropet[:, 1, :], op=ALU.mult),
                          reads=[("ps", 5), "rope"], writes=[("zc", 1)])
                    tr.op("dve", lambda e, dst=dst: e.tensor_tensor(out=dst[0:32, :], in0=zc[:, 0, :], in1=zc[:, 1, :], op=ALU.add),
                          reads=[("zc", 0), ("zc", 1)], writes=[(dkey, "lo")])
                    if j < 8:
                        tr.dma("pool", lambda e, j=j, t=t: e.dma_start(out=qs[:, j, t * T:(t + 1) * T], in_=qb[:, j, :]),
                               reads=[(dkey, "hi"), (dkey, "lo")], writes=[("qs", t, j)])
                    else:
                        tr.dma("pool", lambda e, j=j, t=t: e.dma_start(out=ks[:, j - 8, 128 + t * T:128 + (t + 1) * T], in_=kb[:, j - 8, 0:T]),
                               reads=[(dkey, "hi"), (dkey, "lo")], writes=[("ks", t, j - 8)])
                if stage < 3.3:
                    return
                wv0, rk0 = wload(wb_v[l, 0], ("wb_v", l, 0))
                wv1, rk1 = wload(wb_v[l, 1], ("wb_v", l, 1))
                for b in range(4):
                    bank = 6 + (b % 2)
                    parts = [((xa[:, kc, b * 128:(b + 1) * 128]), (wv0 if kc < 8 else wv1)[:, kc % 8, :]) for kc in range(NCH)]
                    mm_group(ps[:, bank, 0:256], ("ps", bank), parts, reads=[rk0, rk1] + xa_keys)
                    tr.op("act", lambda e, b=b, bank=bank: e.activation(out=vb[:, b, :, :].rearrange("p g d -> p (g d)"), in_=ps[:, bank, 0:256], func=AF.Copy),
                          reads=[("ps", bank)], writes=[("v", b)])
                    tr.dma("pool", lambda e, b=b, t=t: e.dma_start(out=vs[128 + t * T + b * 128:128 + t * T + (b + 1) * 128, :, :], in_=vb[:, b, :, :]),
                           reads=[("v", b)], writes=[("vs", t, b)])
                if stage < 3.4:
                    return
                for j in range(8):
                    wa, rka = wload(wb_in[l, 10 + j], ("wb_in", l, 10 + j))
                    wbb, rkb = wload(wb_in[l, 18 + j], ("wb_in", l, 18 + j))
                    ba, bb = (0, 1) if j % 2 == 0 else (2, 3)
                    mm_group(ps[:, ba, :], ("ps", ba), [(wa[:, kc, :], xa[:, kc, :]) for kc in range(NCH)], reads=[rka] + xa_keys)
                    mm_group(ps[:, bb, :], ("ps", bb), [(wbb[:, kc, :], xa[:, kc, :]) for kc in range(NCH)], reads=[rkb] + xa_keys)
                    sl = j % 2
                    tr.op("act", lambda e, bb=bb, sl=sl: e.activation(out=tmpf[:, sl, :], in_=ps[:, bb, :], func=AF.Sigmoid),
                          reads=[("ps", bb)], writes=[("tmpf", sl)])
                    tr.op("dve", lambda e, ba=ba, sl=sl, j=j: e.tensor_tensor(out=ctmp[:, j, :], in0=ps[:, ba, :], in1=tmpf[:, sl, :], op=ALU.mult),
                          reads=[("ps", ba), ("tmpf", sl)], writes=[("ctmp", j)])
                    tr.dma("pool", lambda e, j=j, t=t: e.dma_start(out=gs[:, j, 16 + t * T:16 + (t + 1) * T], in_=ctmp[:, j, :]),
                           reads=[("ctmp", j)], writes=[("gs", t, j)])

            for t_ in range(NT):
                pass1_tile(t_)

            def flags(t):
                fl = vecs[:, NL * V_LAYER + 16 + 2 * t:NL * V_LAYER + 16 + 2 * t + 1]
                fr = vecs[:, NL * V_LAYER + 16 + 2 * t + 1:NL * V_LAYER + 16 + 2 * t + 2]
                return fl, fr

            def conv_gen(t):
                fl, fr = flags(t)
                greads = [("gs", tt, j) for tt in (t - 1, t, t + 1) if 0 <= tt < NT for j in range(8)]
                if t == 0:
                    greads.append(("gs", -1))
                if t == NT - 1:
                    greads.append(("gs", NT))
                tr.dma("pool", lambda e: e.dma_start(out=gw[:, :, :], in_=gs[:, :, t * T:t * T + 544]), reads=greads, writes=["gw"])
                tr.op("dve", lambda e: e.tensor_scalar(out=gw[:, :, 0:16], in0=gw[:, :, 0:16], scalar1=fl, scalar2=None, op0=ALU.mult),
                      reads=["gw", "vecs"], writes=["gw"])
                tr.op("dve", lambda e: e.tensor_scalar(out=gw[:, :, 528:544], in0=gw[:, :, 528:544], scalar1=fr, scalar2=None, op0=ALU.mult),
                      reads=["gw", "vecs"], writes=["gw"])
                yield
                for k in range(CONVK):
                    for ch in range(8):
                        wcol = vecs[:, l * V_LAYER + V_CW + ch * CONVK + k:l * V_LAYER + V_CW + ch * CONVK + k + 1]
                        src = gw[:, ch, k + 1:k + 1 + T]
                        if k == 0:
                            tr.op("dve", lambda e, ch=ch, wcol=wcol, src=src: e.tensor_scalar(out=acc[:, ch, :], in0=src, scalar1=wcol, scalar2=vcol(l, V_CB, ch),
                                                                                         op0=ALU.mult, op1=ALU.add),
                                  reads=["gw", "vecs"], writes=[("acc", ch), ("accs", ch // 4)])
                        else:
                            tr.op("dve", lambda e, ch=ch, wcol=wcol, src=src: e.scalar_tensor_tensor(out=acc[:, ch, :], in0=src, scalar=wcol, in1=acc[:, ch, :],
                                                                                                op0=ALU.mult, op1=ALU.add),
                                  reads=["gw", "vecs", ("acc", ch)], writes=[("acc", ch)])
                        yield
                for ch in range(8):
                    tr.op("act", lambda e, ch=ch: e.activation(out=ctmp[:, ch, :], in_=acc[:, ch, :], func=AF.Copy),
                          reads=[("acc", ch)], writes=[("ctmp", ch)])
                    yield
                mm_group(ps[:, 6, :], ("ps", 6), [(onesc[:, :], ctmp[:, ch, :]) for ch in range(8)], reads=[("ctmp", ch) for ch in range(8)])
                yield
                for ch in range(8):
                    tr.op("dve", lambda e, ch=ch: e.tensor_tensor(out=acc[:, ch, :], in0=acc[:, ch, :], in1=ps[:, 6, :], op=ALU.subtract),
                          reads=[("acc", ch), ("ps", 6)], writes=[("acc", ch)])
                    yield
                norm_stats(lambda c: ("acc", c), lambda c: acc[:, c, :], ctmp, lambda c: ("ctmp", c), 8, onesc, 1, pbank=7)
                yield
                for ch in range(8):
                    tr.op("dve", lambda e, ch=ch: e.tensor_tensor(out=acc[:, ch, :], in0=acc[:, ch, :], in1=rstd[:, 1, :], op=ALU.mult),
                          reads=[("acc", ch), ("rstd", 1)], writes=[("acc", ch)])
                    tr.op("act", lambda e, ch=ch: e.activation(out=acc[:, ch, :], in_=acc[:, ch, :], func=AF.Silu,
                                                               scale=vcol(l, V_LNG, ch), bias=vcol(l, V_LNB, ch)),
                          reads=[("acc", ch), "vecs"], writes=[("acc", ch)])
                    yield
                norm_stats(lambda c: ("acc", c), lambda c: acc[:, c, :], ctmp, lambda c: ("ctmp", c), 8, onesc, 1, pbank=6)
                yield
                for ch in range(8):
                    tr.op("dve", lambda e, ch=ch: e.scalar_tensor_tensor(out=xa[:, 8 + ch, :], in0=acc[:, ch, :], scalar=vcol(l, V_GCO, ch), in1=rstd[:, 1, :],
                                                                          op0=ALU.mult, op1=ALU.mult),
                          reads=[("acc", ch), ("rstd", 1), "vecs"], writes=[("xa", 8 + ch)])
                    yield

            cstate = {"gen": None}

            def pump(n):
                g_ = cstate["gen"]
                if g_ is None:
                    return
                for _ in range(n):
                    try:
                        next(g_)
                    except StopIteration:
                        cstate["gen"] = None
                        return

            def attention(t):
                qkeys = lambda hh: [(("q", hh), "hi"), (("q", hh), "lo")]
                kkeys = lambda g: [(("k", g), "hi"), (("k", g), "lo")]
                items = [(n, g, hh) for n in range(4) for g in range(2) for hh in range(4)]
                SBK = [0, 1, 6]
                LAG = 2
                DENB = [4, 7]
                psT = ps[:, 5, :].bitcast(BF16)

                def rec_qk(i):
                    n, g, hh = items[i]
                    hd = g * 4 + hh
                    sbk = SBK[i % 3]
                    pslot = i % 4

                    def fnqk(e):
                        ins = None
                        for jj in range(3):
                            ins = e.matmul(ps[:, sbk, jj * 128:(jj + 1) * 128], lhsT=kb[:, g, (n + jj) * 128:(n + jj + 1) * 128],
                                           rhs=qb[:, hd, n * 128:(n + 1) * 128], start=True, stop=True)
                        return ins
                    tr.op("pe", fnqk, reads=qkeys(hd) + kkeys(g), writes=[("ps", sbk)])
                    tr.op("act", lambda e: e.activation(out=ptb[:, pslot, :, :].rearrange("p a b -> p (a b)"), in_=ps[:, sbk, 0:384],
                                                        func=AF.Exp, scale=SCALE),
                          reads=[("ps", sbk)], writes=[("pt", pslot)])
                    mv = 1 if n == 0 else (2 if n == 3 else 0)
                    tr.op("dve", lambda e: e.tensor_tensor(out=ptb[:, pslot, :, :].rearrange("p a b -> p (a b)"), in0=ptb[:, pslot, :, :].rearrange("p a b -> p (a b)"),
                                                           in1=m3[:, mv, :], op=ALU.mult),
                          reads=[("pt", pslot), ("m3", mv)], writes=[("pt", pslot)])

                def rec_pv(i):
                    n, g, hh = items[i]
                    hd = g * 4 + hh
                    ob = 2 + g
                    pslot = i % 4

                    def fnpv(e):
                        ins = None
                        for jj in range(3):
                            ins = e.matmul(ps[:, ob, hh * 128:(hh + 1) * 128], lhsT=ptb[:, pslot, jj, :], rhs=vb[:, n + jj, g, :],
                                           start=(jj == 0), stop=(jj == 2))
                        for jj in range(3):
                            ins = e.matmul(ps[:, DENB[g], hd:hd + 1], lhsT=ptb[:, pslot, jj, :], rhs=ones_col,
                                           start=(jj == 0), stop=(jj == 2))
                        return ins
                    wr = [("pso", ob, hh), ("psd", hd)]
                    if hh == 0:
                        wr.append(("ps", ob))
                    if i == 0 or i == 4:
                        wr.append(("ps", DENB[g]))
                    tr.op("pe", fnpv, reads=[("pt", pslot)] + [("v", n + jj) for jj in range(3)], writes=wr)

                def rec_epiA(n, g):
                    ob = 2 + g
                    hs_ = range(4 * g, 4 * g + 4)
                    tr.op("dve", lambda e: e.tensor_tensor(out=small[:, 4 * g:4 * g + 4], in0=ps[:, DENB[g], 4 * g:4 * g + 4], in1=esink[:, 4 * g:4 * g + 4], op=ALU.add),
                          reads=[("psd", hd) for hd in hs_] + ["esink", ("ps", DENB[g])], writes=[("den", g)])
                    tr.op("dve", lambda e: e.reciprocal(out=small[:, 8 + 4 * g:12 + 4 * g], in_=small[:, 4 * g:4 * g + 4]), reads=[("den", g)], writes=[("rden", g)])
                    for hd in hs_:
                        hh = hd % 4
                        tr.op("dve", lambda e, hd=hd, hh=hh: e.tensor_scalar(out=attn[:, hd, :], in0=ps[:, ob, hh * 128:(hh + 1) * 128],
                                                                           scalar1=small[:, 8 + hd:9 + hd], scalar2=None, op0=ALU.mult),
                              reads=[("pso", ob, hh), ("ps", ob), ("rden", g)], writes=[("attn", hd)])

                def rec_epiB(n):
                    tr.op("act", lambda e: e.activation(out=junk[:, :], in_=attn[:, :, :].rearrange("p a b -> p (a b)"), func=AF.Square),
                          reads=[("attn", hd) for hd in range(8)], writes=["junk"])
                    tr.op("dve", lambda e: e.reduce_sum(out=small[:, 16:17], in_=junk[:, :], axis=AX.X),
                          reads=["junk"], writes=["ass"])
                    tr.op("act", lambda e: e.activation(out=small[:, 17:18], in_=small[:, 16:17], func=AF.Sqrt, bias=epsb[:, 0:1], scale=1.0 / 1024),
                          reads=["ass"], writes=["ams"])
                    tr.op("dve", lambda e: e.reciprocal(out=small[:, 18:19], in_=small[:, 17:18]),
                          reads=["ams"], writes=["arstd"])
                    tr.op("dve", lambda e: e.tensor_scalar(out=attn_n[:, :, :], in0=attn[:, :, :], scalar1=small[:, 18:19], scalar2=None, op0=ALU.mult),
                          reads=["arstd"] + [("attn", hd) for hd in range(8)], writes=["attn_n"])

                def rec_tr(n):
                    def fntr(e):
                        ins = None
                        for c in range(8):
                            ins = e.transpose(psT[:, c * 128:(c + 1) * 128], attn_n[:, c, :], ident_b)
                        return ins
                    tr.op("pe", fntr, reads=["attn_n"], writes=[("ps", 5)])
                    for c in range(8):
                        tr.op("act", lambda e, c=c: e.activation(out=xa[:, c, n * 128:(n + 1) * 128], in_=psT[:, c * 128:(c + 1) * 128],
                                                                 func=AF.Copy, scale=vcol(l, V_GAO, c)),
                              reads=[("ps", 5), "vecs"], writes=[("xa", c)])

                pend = {}
                for i in range(LAG):
                    rec_qk(i)
                for i in range(32):
                    rec_pv(i)
                    if i + LAG < 32:
                        rec_qk(i + LAG)
                    n, g, hh = items[i]
                    if hh == 3:
                        rec_epiA(n, g)
                        if g == 1:
                            rec_epiB(n)
                            pend[min(i + 5, 31)] = n
                    if i in pend:
                        rec_tr(pend.pop(i))

            def pass2_tile(t):
                issue_casts(casts_per_tile)
                fl, fr = flags(t)
                def qkv_loads(t):
                  tr.dma("pool", lambda e, t=t: e.dma_start(out=qb[:, :, :], in_=qs[:, :, t * T:(t + 1) * T]),
                         reads=[("qs", t, j) for j in range(8)], writes=[(("q", j), s_) for j in range(8) for s_ in ("hi", "lo")])
                  kreads = [("ks", tt, g) for tt in (t - 1, t, t + 1) if 0 <= tt < NT for g in range(2)]
                  if t == 0:
                      kreads.append(("ks", -1))
                  if t == NT - 1:
                      kreads.append(("ks", NT))
                  tr.dma("pool", lambda e, t=t: e.dma_start(out=kb[:, :, :], in_=ks[:, :, t * T:t * T + 768]),
                         reads=kreads, writes=[(("k", g), s_) for g in range(2) for s_ in ("hi", "lo")])
                  vreads = [("vs", tt, b) for tt in (t - 1, t, t + 1) if 0 <= tt < NT for b in range(4)]
                  if t == 0:
                      vreads.append(("vs", -1))
                  if t == NT - 1:
                      vreads.append(("vs", NT))
                  tr.dma("pool", lambda e, t=t: e.dma_start(out=vb[:, :, :, :], in_=vs[t * T:t * T + 768, :, :].rearrange("(b p) g d -> p b g d", p=128)),
                         reads=vreads, writes=[("v", b) for b in range(4)] + [("v", 4), ("v", 5)])
                if t == 0:
                    qkv_loads(0)
                tr.dma("pool", lambda e, t=t: e.dma_start(out=h[:, :, :], in_=hin[:, :, t * T:(t + 1) * T]),
                       reads=[("hs", hin_i, t)], writes=[("h", c) for c in range(NCH)])
                tr.dma("pool", lambda e, t=t: e.dma_start(out=ptok[:, :, :], in_=p_d[l, t * T:(t + 1) * T, :].rearrange("(b p) f -> p b f", p=128)),
                       writes=["ptok"])
                tr.op("dve", lambda e, fl=fl: e.tensor_scalar(out=m3[:, 1, 0:128], in0=tri_ge, scalar1=fl, scalar2=None, op0=ALU.mult),
                      reads=["vecs"], writes=[("m3", 1)])
                tr.op("dve", lambda e, fr=fr: e.tensor_scalar(out=m3[:, 2, 256:384], in0=tri_le, scalar1=fr, scalar2=None, op0=ALU.mult),
                      reads=["vecs"], writes=[("m3", 2)])
                if t == 0:
                    cstate["gen"] = conv_gen(0)
                pump(10 ** 9)

                attention(t)
                if t + 1 < NT:
                    qkv_loads(t + 1)

                xa_keys = [("xa", c) for c in range(NCH)]
                if dbg and t == 0 and l == 0:
                    tr.dma("pool", lambda e: e.dma_start(out=dbg_mixed[:, :, :], in_=xa[:, :, :]), reads=xa_keys, writes=["dbgm"])
                for j in range(NCH):
                    wv, rk = wload(wb_out[l, j], ("wb_out", l, j))
                    bank = j % 4
                    mm_group(ps[:, bank, :], ("ps", bank), [(wv[:, kc, :], xa[:, kc, :]) for kc in range(NCH)], reads=[rk] + xa_keys)
                    tr.op("dve", lambda e, j=j, bank=bank: e.tensor_tensor(out=h[:, j, :], in0=h[:, j, :], in1=ps[:, bank, :], op=ALU.add),
                          reads=[("ps", bank), ("h", j)], writes=[("h", j)])
                if dbg and t == 0 and l == 0:
                    tr.dma("pool", lambda e: e.dma_start(out=dbg_h1[:, :, :], in_=h[:, :, :]), reads=[("h", c) for c in range(NCH)], writes=["dbgh1"])
                norm_stats(lambda c: ("h", c), lambda c: h[:, c, :], xb, lambda c: ("xb", c), NCH, onesd, 0)
                for c in range(NCH):
                    tr.op("dve", lambda e, c=c: e.scalar_tensor_tensor(out=xb[:, c, :], in0=h[:, c, :], scalar=vcol(l, V_GFFN, c), in1=rstd[:, 0, :],
                                                                        op0=ALU.mult, op1=ALU.mult),
                          reads=[("h", c), ("rstd", 0), "vecs"], writes=[("xb", c)])
                if t + 1 < NT:
                    cstate["gen"] = conv_gen(t + 1)
                xb_keys = [("xb", c) for c in range(NCH)]
                for half in range(2):
                    for jj in range(NFH):
                        j = half * NFH + jj
                        wg, rkg = wload(wb_gate[l, j], ("wb_gate", l, j))
                        wu, rku = wload(wb_up[l, j], ("wb_up", l, j))
                        ba, bb = (0, 1) if jj % 2 == 0 else (2, 3)
                        if j == 0:
                            mm_group(ps[:, ba, :], ("ps", ba), [(wg[:, kc, :], xb[:, kc, :]) for kc in range(NCH)], reads=[rkg],
                                     part_reads=[[("xb", kc)] for kc in range(NCH)])
                        else:
                            mm_group(ps[:, ba, :], ("ps", ba), [(wg[:, kc, :], xb[:, kc, :]) for kc in range(NCH)], reads=[rkg] + xb_keys)
                        mm_group(ps[:, bb, :], ("ps", bb), [(wu[:, kc, :], xb[:, kc, :]) for kc in range(NCH)], reads=[rku] + xb_keys)
                        sl = jj % 2
                        tr.op("act", lambda e, ba=ba, sl=sl: e.activation(out=tmpf[:, sl, :], in_=ps[:, ba, :], func=AF.Silu),
                              reads=[("ps", ba)], writes=[("tmpf", sl)])
                        tr.op("dve", lambda e, bb=bb, sl=sl, jj=jj: e.tensor_tensor(out=act[:, jj, :], in0=ps[:, bb, :], in1=tmpf[:, sl, :], op=ALU.mult),
                              reads=[("ps", bb), ("tmpf", sl)], writes=[("act", jj)])
                        pump(6)
                    act_keys = [("act", jj) for jj in range(NFH)]
                    for j in range(NCH):
                        w0, rk0_ = wload(wb_down[l, j, half * 2], ("wb_down", l, j, half * 2))
                        w1, rk1_ = wload(wb_down[l, j, half * 2 + 1], ("wb_down", l, j, half * 2 + 1))
                        bank = j % 4
                        parts = [((w0 if kk < 11 else w1)[:, kk % 11, :], act[:, kk, :]) for kk in range(NFH)]
                        mm_group(ps[:, bank, :], ("ps", bank), parts, reads=[rk0_, rk1_] + act_keys)
                        tr.op("dve", lambda e, j=j, bank=bank: e.tensor_tensor(out=h[:, j, :], in0=h[:, j, :], in1=ps[:, bank, :], op=ALU.add),
                              reads=[("ps", bank), ("h", j)], writes=[("h", j)])
                        pump(3)
                if dbg and t == 0 and l == 0:
                    tr.dma("pool", lambda e: e.dma_start(out=dbg_h2[:, :, :], in_=h[:, :, :]), reads=[("h", c) for c in range(NCH)], writes=["dbgh2"])
                for c2 in range(2):
                    def fnpt(e, c2=c2):
                        ins = None
                        for b in range(4):
                            ins = e.transpose(ps[:, 6 + c2, b * 128:(b + 1) * 128], ptok[:, b, c2 * 128:(c2 + 1) * 128], ident_f)
                        return ins
                    tr.op("pe", fnpt, reads=["ptok"], writes=[("ps", 6 + c2)])
                    tr.op("act", lambda e, c2=c2: e.activation(out=pT[:, c2, :], in_=ps[:, 6 + c2, :], func=AF.Copy),
                          reads=[("ps", 6 + c2)], writes=[("pT", c2)])
                norm_stats(lambda c: ("h", c), lambda c: h[:, c, :], xb, lambda c: ("xb", c), NCH, onesd, 0)
                for c in range(NCH):
                    tr.op("dve", lambda e, c=c: e.scalar_tensor_tensor(out=xb[:, c, :], in0=h[:, c, :], scalar=vcol(l, V_GPLE, c), in1=rstd[:, 0, :],
                                                                        op0=ALU.mult, op1=ALU.mult),
                          reads=[("h", c), ("rstd", 0), "vecs"], writes=[("xb", c)])
                wp0, wp1, rkp0, rkp1 = wple[:, 0, :], wple[:, 1, :], "wple", "wple"
                for j in range(NCH):
                    wv, rk = wload(wb_pg[l, j], ("wb_pg", l, j))
                    ba, bb = (0, 1) if j % 2 == 0 else (2, 3)
                    if j == 0:
                        mm_group(ps[:, ba, :], ("ps", ba), [(wv[:, kc, :], xb[:, kc, :]) for kc in range(NCH)], reads=[rk],
                                 part_reads=[[("xb", kc)] for kc in range(NCH)])
                    else:
                        mm_group(ps[:, ba, :], ("ps", ba), [(wv[:, kc, :], xb[:, kc, :]) for kc in range(NCH)], reads=[rk] + xb_keys)
                    mm_group(ps[:, bb, :], ("ps", bb), [(wp0[:, j * 128:(j + 1) * 128], pT[:, 0, :]), (wp1[:, j * 128:(j + 1) * 128], pT[:, 1, :])],
                             reads=[rkp0, rkp1, ("pT", 0), ("pT", 1)])
                    sl = j % 2
                    tr.op("act", lambda e, ba=ba, sl=sl: e.activation(out=tmpf[:, sl, :], in_=ps[:, ba, :], func=AF.Sigmoid),
                          reads=[("ps", ba)], writes=[("tmpf", sl)])
                    tr.op("dve", lambda e, bb=bb, sl=sl: e.tensor_tensor(out=tmpf[:, sl, :], in0=ps[:, bb, :], in1=tmpf[:, sl, :], op=ALU.mult),
                          reads=[("ps", bb), ("tmpf", sl)], writes=[("tmpf", sl)])
                    tr.op("dve", lambda e, j=j, sl=sl: e.tensor_tensor(out=h[:, j, :], in0=h[:, j, :], in1=tmpf[:, sl, :], op=ALU.add),
                          reads=[("tmpf", sl), ("h", j)], writes=[("h", j)])
                    pump(3)
                pump(10 ** 9)
                if l + 1 < NL:
                    tr.dma("pool", lambda e, t=t: e.dma_start(out=hout[:, :, t * T:(t + 1) * T], in_=h[:, :, :]),
                           reads=[("h", c) for c in range(NCH)], writes=[("hs", hout_i, t)])
                else:
                    norm_stats(lambda c: ("h", c), lambda c: h[:, c, :], xb, lambda c: ("xb", c), NCH, onesd, 0)
                    gf0 = NL * V_LAYER
                    for c in range(NCH):
                        tr.op("dve", lambda e, c=c: e.scalar_tensor_tensor(out=h[:, c, :], in0=h[:, c, :], scalar=vecs[:, gf0 + c:gf0 + c + 1], in1=rstd[:, 0, :],
                                                                            op0=ALU.mult, op1=ALU.mult),
                              reads=[("h", c), ("rstd", 0), "vecs"], writes=[("h", c)])
                    for b in range(4):
                        half = b % 2
                        stg = xb[:, half * 8:(half + 1) * 8, :].rearrange("p a b -> p (a b)").bitcast(F32)
                        skey = ("xbs", half)
                        xbk = [("xb", half * 8 + i) for i in range(8)]
                        for c4 in range(4):
                            bank = 6 + (c4 % 2)

                            def fnty(e, b=b, c4=c4, bank=bank):
                                ins = None
                                for cc in range(4):
                                    c = c4 * 4 + cc
                                    ins = e.transpose(ps[:, bank, cc * 128:(cc + 1) * 128], h[:, c, b * 128:(b + 1) * 128], ident_f)
                                return ins
                            tr.op("pe", fnty, reads=[("h", c4 * 4 + cc) for cc in range(4)], writes=[("ps", bank)])
                            if c4 % 2 == 0:
                                tr.op("act", lambda e, c4=c4, bank=bank, stg=stg: e.activation(out=stg[:, c4 * 512:(c4 + 1) * 512], in_=ps[:, bank, :], func=AF.Copy),
                                      reads=[("ps", bank)], writes=([skey] + xbk) if c4 == 0 else [(skey, c4)])
                            else:
                                tr.op("dve", lambda e, c4=c4, bank=bank, stg=stg: e.tensor_copy(out=stg[:, c4 * 512:(c4 + 1) * 512], in_=ps[:, bank, :]),
                                      reads=[("ps", bank)], writes=[(skey, c4)])
                        tok0 = t * T + b * 128
                        tr.dma("pool", lambda e, stg=stg, tok0=tok0: e.dma_start(out=y_d[tok0:tok0 + 128, :], in_=stg),
                               reads=[skey] + [(skey, c4) for c4 in range(1, 4)], writes=[("y", t, b), skey] + xbk)

            for t_ in range(NT if stage >= 4 else 0):
                pass2_tile(t_)

        for l_ in range(NL if stage >= 3 else 0):
            do_layer(l_)

        tr.sync_all()
        tr.emit(block)
    return nc


def _consts():
    c = np.zeros((128, 5 * 128), np.float32)
    c[:, 0:128] = np.eye(128, dtype=np.float32)
    kk = np.arange(128)[:, None]
    tt = np.arange(128)[None, :]
    c[:, 128:256] = (kk >= tt).astype(np.float32)
    c[:, 256:384] = (kk <= tt).astype(np.float32)
    for m in range(32):
        c[(m + 16) % 32, 384 + m] = 1.0
    c[:, 512:640] = 1.0
    return c


def _rope_tables(pos):
    half = 16
    inv_freq = np.exp(np.float32(-math.log(ROPE_THETA)) * np.arange(0, 32, 2, dtype=np.float32) / np.float32(32)).astype(np.float32)
    ang = (pos.astype(np.float32)[:, None] * inv_freq[None, :]).astype(np.float32)
    cos = np.cos(ang).astype(np.float32).T
    sin = np.sin(ang).astype(np.float32).T
    nt = pos.shape[0] // T
    out = np.zeros((nt, 2, 32, T), np.float32)
    for t in range(nt):
        sl = slice(t * T, (t + 1) * T)
        out[t, 0, 0:16] = cos[:, sl]
        out[t, 0, 16:32] = cos[:, sl]
        out[t, 1, 0:16] = -sin[:, sl]
        out[t, 1, 16:32] = sin[:, sl]
    return out


def _pack_vecs(inp, NL, NT, flags):
    NV = NL * V_LAYER + 16 + 2 * NT
    v = np.zeros((128, NV), np.float32)

    def fm(a, n):
        return np.asarray(a, np.float32).reshape(n, 128).T
    for l in range(NL):
        o = l * V_LAYER
        v[:, o + V_GMIX:o + V_GMIX + 16] = fm(inp["g_mix"][l], 16)
        v[:, o + V_GFFN:o + V_GFFN + 16] = fm(inp["g_ffn"][l], 16)
        v[:, o + V_GPLE:o + V_GPLE + 16] = fm(inp["g_ple"][l], 16)
        v[:, o + V_CB:o + V_CB + 8] = fm(inp["conv_b"][l], 8)
        v[:, o + V_LNG:o + V_LNG + 8] = fm(inp["conv_ln_g"][l], 8)
        v[:, o + V_LNB:o + V_LNB + 8] = fm(inp["conv_ln_b"][l], 8)
        v[:, o + V_GCO:o + V_GCO + 8] = fm(inp["g_conv_out"][l], 8)
        v[:, o + V_GAO:o + V_GAO + 8] = fm(inp["g_attn_out"][l], 8)
        cw = np.asarray(inp["conv_w"][l], np.float32)
        cwt = cw.T.reshape(8, 128, CONVK).transpose(1, 0, 2).reshape(128, 8 * CONVK)
        v[:, o + V_CW:o + V_CW + 8 * CONVK] = cwt
        v[:, o + V_SINK:o + V_SINK + 8] = np.asarray(inp["sink"][l], np.float32)[None, :]
    o = NL * V_LAYER
    v[:, o:o + 16] = fm(inp["g_final"], 16)
    v[:, o + 16:o + 16 + 2 * NT] = flags.reshape(1, 2 * NT)
    return v


_PROG_CACHE = {}


def _run(core_inputs, NT, NL):
    key = (NT, NL)
    if key not in _PROG_CACHE:
        _PROG_CACHE[key] = build_program(NT, NL)
    nc = _PROG_CACHE[key]
    res = run_bass_kernel_spmd(nc, core_inputs, core_ids=list(range(len(core_inputs))))
    return [r["y"] for r in res.results]


def kernel(**inputs):
    inp = {k: np.asarray(v) for k, v in inputs.items()}
    NL, NT = 4, NT_FULL
    xp, xsamp = inp["x_prompt"], inp["x_sample"]
    pp, psamp = inp["p_prompt"], inp["p_sample"]
    consts = _consts()
    wkeys = ["w_in", "w_out", "w_gate", "w_up", "w_down", "w_ple_gate", "w_ple"]
    shared = {k: np.ascontiguousarray(inp[k], dtype=np.float32) for k in wkeys}
    NTOK = NT * T
    core_inputs = []
    for c in range(N_CORES):
        flags = np.ones((NT, 2), np.float32)
        flags[0, 0] = 0.0
        flags[NT - 1, 1] = 0.0
        if c < 4:
            b, half = c // 2, c % 2
            s0 = 0 if half == 0 else 16384 - NTOK
            xc = xp[b, s0:s0 + NTOK]
            pc = pp[:, b, s0:s0 + NTOK]
            pos = np.arange(s0, s0 + NTOK, dtype=np.float32)
        else:
            s = 2 * (c - 4)
            xc = np.zeros((NTOK, D), np.float32)
            pc = np.zeros((NL, NTOK, PLE), np.float32)
            xc[0:4096] = xsamp[s]
            xc[4096:8192] = xsamp[s + 1]
            pc[:, 0:4096] = psamp[:, s]
            pc[:, 4096:8192] = psamp[:, s + 1]
            pos = np.concatenate([np.arange(4096), np.arange(4096), np.arange(NTOK - 8192)]).astype(np.float32)
            flags[7, 1] = 0.0
            flags[8, 0] = 0.0
            flags[15, 1] = 0.0
            flags[16, 0] = 0.0
        m = dict(shared)
        m["x"] = np.ascontiguousarray(xc, dtype=np.float32)
        m["p"] = np.ascontiguousarray(pc, dtype=np.float32)
        m["vecs"] = _pack_vecs(inp, NL, NT, flags)
        m["rope"] = _rope_tables(pos)
        m["consts"] = consts
        core_inputs.append(m)
    ys = _run(core_inputs, NT, NL)
    y_prompt = np.empty((2, 16384, D), np.float32)
    y_sample = np.empty((8, 4096, D), np.float32)
    for c in range(N_CORES):
        yc = ys[c]
        if c < 4:
            b, half = c // 2, c % 2
            if half == 0:
                y_prompt[b, 0:8192] = yc[0:8192]
            else:
                y_prompt[b, 8192:16384] = yc[NTOK - 8192:NTOK]
        else:
            s = 2 * (c - 4)
            y_sample[s] = yc[0:4096]
            y_sample[s + 1] = yc[4096:8192]
    return (y_prompt, y_sample)
```

```python
import math
import numpy as np
import concourse.bass as bass
import concourse.mybir as mybir
from concourse.bass_utils import run_bass_kernel_spmd

F32 = mybir.dt.float32
BF16 = mybir.dt.bfloat16
AF = mybir.ActivationFunctionType
ALU = mybir.AluOpType
AX = mybir.AxisListType

D = 2048
NCH = 16
T = 512
DFF = 5632
NFF = 44
NFH = 22
PLE = 256
CONVK = 31
EPS = 1e-6
ROPE_THETA = 500000.0
N_CORES = 8
NT_FULL = 17
NW_SLOTS = 5
SCALE = 128 ** -0.5

V_GMIX, V_GFFN, V_GPLE, V_CB, V_LNG, V_LNB, V_GCO, V_GAO, V_CW, V_SINK = 0, 16, 32, 48, 56, 64, 72, 80, 88, 336
V_LAYER = 344


class Tracker:
    def __init__(self, nc, sems):
        self.nc = nc
        self.engs = {"pe": nc.tensor, "act": nc.scalar, "dve": nc.vector, "pool": nc.gpsimd, "sp": nc.sync}
        self.ops = {e: [] for e in self.engs}
        self.cnt = {e: 0 for e in ("pe", "act", "dve", "pool")}
        self.sem = sems
        self.res = {}
        self.known = {e: {} for e in self.engs}
        self.dq = {"sp": [f"sp{i}" for i in range(8)], "pool": [f"pl{i}" for i in range(8)]}
        self.dqi = {"sp": 0, "pool": 0}
        self.dcnt = {}
        for q in self.dq.values():
            for s in q:
                self.dcnt[s] = 0
        self.untracked = set()

    def _deps(self, reads, writes):
        deps = {}

        def add(tok):
            if tok is None:
                return
            s, v = tok
            if deps.get(s, 0) < v:
                deps[s] = v
        for k in reads:
            if k in self.untracked:
                continue
            r = self.res.get(k)
            if r:
                add(r[0])
        for k in writes:
            r = self.res.get(k)
            if r:
                add(r[0])
                for t in r[1]:
                    add(t)
        return deps

    def _commit(self, eng, deps, fn, tok, reads, writes):
        kn = self.known[eng]
        waits = []
        for s, v in deps.items():
            if kn.get(s, 0) < v:
                waits.append((s, v))
                kn[s] = v
        self.ops[eng].append((waits, fn, tok))
        for k in reads:
            if k in self.untracked:
                continue
            r = self.res.get(k)
            if r is None:
                self.res[k] = [None, [tok]]
            else:
                r[1].append(tok)
        for k in writes:
            self.res[k] = [tok, []]

    def op(self, eng, fn, reads=(), writes=()):
        deps = self._deps(reads, writes)
        self.cnt[eng] += 1
        tok = (eng, self.cnt[eng])
        self._commit(eng, deps, fn, tok, reads, writes)
        return tok

    def dma(self, q, fn, reads=(), writes=()):
        deps = self._deps(reads, writes)
        ring = self.dq[q]
        s = ring[self.dqi[q] % len(ring)]
        self.dqi[q] += 1
        prev = self.dcnt[s]
        if prev > 0 and deps.get(s, 0) < prev:
            deps[s] = prev
        self.dcnt[s] = prev + 16
        tok = (s, prev + 16)
        self._commit(q, deps, fn, tok, reads, writes)
        return tok

    def sync_all(self):
        allt = {}
        for e, c in self.cnt.items():
            if c:
                allt[e] = c
        for s, c in self.dcnt.items():
            if c:
                allt[s] = c
        for e in self.engs:
            kn = self.known[e]
            waits = []
            for s, v in allt.items():
                if kn.get(s, 0) < v:
                    waits.append((s, v))
                    kn[s] = v
            if waits:
                self.ops[e].append((waits, None, None))

    def emit(self, block):
        tr = self

        def run(ename, e):
            for waits, fn, tok in tr.ops[ename]:
                for s, v in waits:
                    e.wait_ge(tr.sem[s], v)
                if fn is None:
                    continue
                ins = fn(e)
                s, v = tok
                ins.then_inc(tr.sem[s], 16 if s not in tr.cnt else 1)

        @block.tensor
        def _(e):
            run("pe", e)

        @block.scalar
        def _(e):
            run("act", e)

        @block.vector
        def _(e):
            run("dve", e)

        @block.gpsimd
        def _(e):
            run("pool", e)

        @block.sync
        def _(e):
            run("sp", e)


def build_program(NT=NT_FULL, NL=4, dbg=False, stage=99):
    nc = bass.Bass("TRN2", target_bir_lowering=False)
    NTOK = NT * T

    def din(name, shape, dt=F32):
        return nc.dram_tensor(name, list(shape), dt, kind="ExternalInput").ap()

    def dscr(name, shape, dt):
        return nc.dram_tensor(name, list(shape), dt, kind="Internal").ap()

    x_d = din("x", [NTOK, D])
    p_d = din("p", [NL, NTOK, PLE])
    w_in_d = din("w_in", [NL, D, 3584])
    w_out_d = din("w_out", [NL, D, D])
    w_gate_d = din("w_gate", [NL, D, DFF])
    w_up_d = din("w_up", [NL, D, DFF])
    w_down_d = din("w_down", [NL, DFF, D])
    w_pg_d = din("w_ple_gate", [NL, D, D])
    w_pl_d = din("w_ple", [NL, PLE, D])
    NV = NL * V_LAYER + 16 + 2 * NT
    vecs_d = din("vecs", [128, NV])
    rope_d = din("rope", [NT, 2, 32, T])
    cst_d = din("consts", [128, 5 * 128])
    y_d = nc.dram_tensor("y", [NTOK, D], F32, kind="ExternalOutput").ap()

    hs = [dscr(f"hs{i}", [128, NCH, NTOK], F32) for i in range(2)]
    qs = dscr("qs", [128, 8, NTOK], BF16)
    ks = dscr("ks", [128, 2, NTOK + 256], BF16)
    vs = dscr("vs", [NTOK + 256, 2, 128], BF16)
    gs = dscr("gs", [128, 8, NTOK + 32], BF16)
    wb_in = dscr("wb_in", [NL, 26, 128, 16, 128], BF16)
    wb_v = dscr("wb_v", [NL, 2, 128, 8, 256], BF16)
    wb_out = dscr("wb_out", [NL, 16, 128, 16, 128], BF16)
    wb_gate = dscr("wb_gate", [NL, NFF, 128, 16, 128], BF16)
    wb_up = dscr("wb_up", [NL, NFF, 128, 16, 128], BF16)
    wb_down = dscr("wb_down", [NL, 16, 4, 128, 11, 128], BF16)
    wb_pg = dscr("wb_pg", [NL, 16, 128, 16, 128], BF16)
    wb_pl = dscr("wb_pl", [NL, 2, 128, D], BF16)

    if dbg:
        dbg_mixed = dscr("dbg_mixed", [128, NCH, T], BF16)
        dbg_h1 = dscr("dbg_h1", [128, NCH, T], F32)
        dbg_h2 = dscr("dbg_h2", [128, NCH, T], F32)
    sem_names = ["pe", "act", "dve", "pool"] + [f"sp{i}" for i in range(8)] + [f"pl{i}" for i in range(8)]

    import contextlib
    with contextlib.ExitStack() as es:
        def sb(name, shape, dt):
            return es.enter_context(nc.sbuf_tensor("sb_" + name, list(shape), dt))

        h = sb("h", [128, NCH, T], F32)
        xa = sb("xa", [128, NCH, T], BF16)
        xb = sb("xb", [128, NCH, T], BF16)
        qb = sb("qb", [128, 8, T], BF16)
        kb = sb("kb", [128, 2, 768], BF16)
        vb = sb("vb", [128, 6, 2, 128], BF16)
        gw = sb("gw", [128, 8, 544], BF16)
        acc = sb("acc", [128, 8, T], F32)
        ctmp = sb("ctmp", [128, 8, T], BF16)
        act = sb("act", [128, NFH, T], BF16)
        ptb = sb("ptb", [128, 4, 3, 128], BF16)
        attn = sb("attn", [128, 8, 128], F32)
        attn_n = sb("attn_n", [128, 8, 128], BF16)
        junk = sb("junk", [128, 1024], BF16)
        small = sb("small", [128, 64], F32)
        ptok = sb("ptok", [128, 4, PLE], F32)
        pT = sb("pT", [128, 2, T], BF16)
        rstd = sb("rstd", [128, 2, T], F32)
        tmpf = sb("tmpf", [128, 2, T], F32)
        ropet = sb("ropet", [32, 2, T], F32)
        zr = sb("zr", [32, T], BF16)
        zc = sb("zc", [32, 2, T], F32)
        wring = sb("wring", [128, NW_SLOTS, 2048], BF16)
        wple = sb("wple", [128, 2, D], BF16)
        vecs = sb("vecs", [128, NV], F32)
        cstf = sb("cstf", [128, 5 * 128], F32)
        cstb = sb("cstb", [128, 5 * 128], BF16)
        onesd = sb("onesd", [128, 128], BF16)
        onesc = sb("onesc", [128, 128], BF16)
        m3 = sb("m3", [128, 3, 384], BF16)
        esink = sb("esink", [128, 8], F32)
        zpad = sb("zpad", [128, 2, 128], BF16)
        zg = sb("zg", [128, 8, 16], BF16)
        epsb = sb("epsb", [128, 1], F32)
        ps = es.enter_context(nc.psum_tensor("ps", [128, 8, T], F32))
        sems = {n: es.enter_context(nc.semaphore(n)) for n in sem_names}
        print("sbuf bytes remaining", nc.sbuf_bytes_remaining, flush=True)
        block = es.enter_context(nc.Block())

        tr = Tracker(nc, sems)
        ident_f = cstf[:, 0:128]
        ident_b = cstb[:, 0:128]
        tri_ge = cstb[:, 128:256]
        tri_le = cstb[:, 256:384]
        pswap = cstb[0:32, 384:416]
        ones_col = cstb[:, 512:513]

        tr.dma("pool", lambda e: e.dma_start(out=vecs[:, :], in_=vecs_d[:, :]), writes=["vecs"])
        tr.dma("pool", lambda e: e.dma_start(out=cstf[:, :], in_=cst_d[:, :]), writes=["cstf"])
        tr.op("dve", lambda e: e.tensor_copy(out=cstb[:, :], in_=cstf[:, :]), reads=["cstf"], writes=["cstb"])
        for v_ in range(3):
            tr.op("dve", lambda e, v_=v_: e.tensor_copy(out=m3[:, v_, 0:128], in_=cstb[:, 128:256]), reads=["cstb"], writes=[("m3", v_)])
            tr.op("dve", lambda e, v_=v_: e.tensor_copy(out=m3[:, v_, 128:256], in_=cstb[:, 512:640]), reads=["cstb"], writes=[("m3", v_)])
            tr.op("dve", lambda e, v_=v_: e.tensor_copy(out=m3[:, v_, 256:384], in_=cstb[:, 256:384]), reads=["cstb"], writes=[("m3", v_)])
        tr.op("dve", lambda e: e.memset(onesd[:, :], 1.0 / D), writes=["onesd"])
        tr.op("dve", lambda e: e.memset(onesc[:, :], 1.0 / 1024), writes=["onesc"])
        tr.op("dve", lambda e: e.memset(zpad[:, :, :], 0.0), writes=["zpad"])
        tr.op("dve", lambda e: e.memset(zg[:, :, :], 0.0), writes=["zg"])
        tr.op("dve", lambda e: e.memset(epsb[:, :], EPS), writes=["epsb"])
        tr.dma("pool", lambda e: e.dma_start(out=ks[:, :, 0:128], in_=zpad[:, :, :]), reads=["zpad"], writes=[("ks", -1)])
        tr.dma("pool", lambda e: e.dma_start(out=ks[:, :, NTOK + 128:NTOK + 256], in_=zpad[:, :, :]), reads=["zpad"], writes=[("ks", NT)])
        tr.dma("pool", lambda e: e.dma_start(out=vs[0:128, :, :], in_=zpad[:, :, :]), reads=["zpad"], writes=[("vs", -1)])
        tr.dma("pool", lambda e: e.dma_start(out=vs[NTOK + 128:NTOK + 256, :, :], in_=zpad[:, :, :]), reads=["zpad"], writes=[("vs", NT)])
        tr.dma("pool", lambda e: e.dma_start(out=gs[:, :, 0:16], in_=zg[:, :, :]), reads=["zg"], writes=[("gs", -1)])
        tr.dma("pool", lambda e: e.dma_start(out=gs[:, :, NTOK + 16:NTOK + 32], in_=zg[:, :, :]), reads=["zg"], writes=[("gs", NT)])

        tr.sync_all()
        tr.untracked.update(["vecs", "cstb", "cstf", "onesd", "onesc"])
        def cast_list(l):
            lst = []

            def fm(dst, src_cols, key):
                lst.append((key, lambda e, dst=dst, src=src_cols: e.dma_start(
                    out=dst, in_=src.rearrange("(kc p) m -> p kc m", p=128))))
            for j in range(8):
                fm(wb_in[l, j], w_in_d[l, :, j * 128:(j + 1) * 128], ("wb_in", l, j))
            for j in range(2):
                fm(wb_in[l, 8 + j], w_in_d[l, :, 1024 + j * 128:1024 + (j + 1) * 128], ("wb_in", l, 8 + j))
            for j in range(16):
                fm(wb_in[l, 10 + j], w_in_d[l, :, 1536 + j * 128:1536 + (j + 1) * 128], ("wb_in", l, 10 + j))
            for pc in range(2):
                lst.append((("wb_v", l, pc), lambda e, pc=pc: e.dma_start(
                    out=wb_v[l, pc], in_=w_in_d[l, pc * 1024:(pc + 1) * 1024, 1280:1536].rearrange("(kc p) m -> p kc m", p=128))))
            for j in range(16):
                fm(wb_out[l, j], w_out_d[l, :, j * 128:(j + 1) * 128], ("wb_out", l, j))
            for j in range(NFF):
                fm(wb_gate[l, j], w_gate_d[l, :, j * 128:(j + 1) * 128], ("wb_gate", l, j))
                fm(wb_up[l, j], w_up_d[l, :, j * 128:(j + 1) * 128], ("wb_up", l, j))
            for j in range(16):
                for pc in range(4):
                    lst.append((("wb_down", l, j, pc), lambda e, j=j, pc=pc: e.dma_start(
                        out=wb_down[l, j, pc],
                        in_=w_down_d[l, pc * 1408:(pc + 1) * 1408, j * 128:(j + 1) * 128].rearrange("(kc p) m -> p kc m", p=128))))
            for j in range(16):
                fm(wb_pg[l, j], w_pg_d[l, :, j * 128:(j + 1) * 128], ("wb_pg", l, j))
            for kc in range(2):
                lst.append((("wb_pl", l, kc), lambda e, kc=kc: e.dma_start(
                    out=wb_pl[l, kc], in_=w_pl_d[l, kc * 128:(kc + 1) * 128, :])))
            return lst

        pending_casts = []

        def issue_casts(n):
            for _ in range(min(n, len(pending_casts))):
                key, fn = pending_casts.pop(0)
                tr.dma("pool", fn, writes=[key])

        if stage >= 1:
            pending_casts.extend(cast_list(0))
            issue_casts(28)

        wstate = {"n": 0}

        def wload(src_ap, key, shape3=None):
            slot = wstate["n"] % NW_SLOTS
            wstate["n"] += 1
            nel = 1
            for s in src_ap.shape[1:]:
                nel *= s
            dst = wring[:, slot, 0:nel]
            if len(src_ap.shape) == 3:
                dstv = dst.rearrange("p (a b) -> p a b", b=src_ap.shape[2])
            else:
                dstv = dst
            rk = ("wr", slot)
            tr.dma("sp", lambda e: e.dma_start(out=dstv, in_=src_ap), reads=[key], writes=[rk])
            return dstv, rk

        def mm_group(bank_ap, bank_key, parts, reads, extra_writes=(), part_reads=None):
            if part_reads is not None:
                n = len(parts)
                tok = None
                for i, (l_, r_) in enumerate(parts):
                    def fn1(e, i=i, l_=l_, r_=r_):
                        return e.matmul(bank_ap, lhsT=l_, rhs=r_, start=(i == 0), stop=(i == n - 1))
                    wr = [bank_key] + list(extra_writes) if (i == 0 or i == n - 1) else []
                    tok = tr.op("pe", fn1, reads=list(reads) + list(part_reads[i]), writes=wr)
                return tok
            def fn(e):
                ins = None
                n = len(parts)
                for i, (l_, r_) in enumerate(parts):
                    ins = e.matmul(bank_ap, lhsT=l_, rhs=r_, start=(i == 0), stop=(i == n - 1))
                return ins
            return tr.op("pe", fn, reads=reads, writes=[bank_key] + list(extra_writes))

        def vcol(l, base, c):
            o = l * V_LAYER + base + c
            return vecs[:, o:o + 1]

        def norm_stats(src_key_fn, src_ap_fn, sqbuf, sq_key_fn, nchunks, ones, rs_slot, pbank=4):
            for c in range(nchunks):
                rk_ = src_key_fn(c)
                tr.op("act", lambda e, c=c: e.activation(out=sqbuf[:, c, :], in_=src_ap_fn(c), func=AF.Square),
                      reads=(rk_ if isinstance(rk_, list) else [rk_]), writes=[sq_key_fn(c)])
            mm_group(ps[:, pbank, :], ("ps", pbank), [(ones[:, :], sqbuf[:, c, :]) for c in range(nchunks)],
                     reads=[sq_key_fn(c) for c in range(nchunks)])
            if stage < 3.05:
                return
            tr.op("act", lambda e: e.activation(out=rstd[:, rs_slot, :], in_=ps[:, pbank, :], func=AF.Sqrt, bias=epsb[:, 0:1], scale=1.0),
                  reads=[("ps", pbank)], writes=[("rstd", rs_slot)])
            tr.op("dve", lambda e: e.reciprocal(out=rstd[:, rs_slot, :], in_=rstd[:, rs_slot, :]),
                  reads=[("rstd", rs_slot)], writes=[("rstd", rs_slot)])

        for t in range(NT if stage >= 2 else 0):
            for b in range(4):
                half = b % 2
                stg = acc[:, half * 4:(half + 1) * 4, :].rearrange("p a b -> p (a b)")
                skey = ("accs", half)
                tok0 = t * T + b * 128
                tr.dma("pool", lambda e, stg=stg, tok0=tok0: e.dma_start(out=stg, in_=x_d[tok0:tok0 + 128, :]), writes=[skey])
                for c in range(NCH):
                    bank = 5 + (c % 2)

                    def fn(e, c=c, bank=bank, stg=stg):
                        return e.transpose(ps[:, bank, 0:128], stg[:, c * 128:(c + 1) * 128], ident_f)
                    tr.op("pe", fn, reads=[skey], writes=[("ps", bank)])
                    eng = "dve" if c % 2 == 0 else "act"
                    if eng == "dve":
                        tr.op("dve", lambda e, c=c, bank=bank, b=b: e.tensor_copy(out=h[:, c, b * 128:(b + 1) * 128], in_=ps[:, bank, 0:128]),
                              reads=[("ps", bank)], writes=[("h", c)])
                    else:
                        tr.op("act", lambda e, c=c, bank=bank, b=b: e.activation(out=h[:, c, b * 128:(b + 1) * 128], in_=ps[:, bank, 0:128], func=AF.Copy),
                              reads=[("ps", bank)], writes=[("h", c)])
            tr.dma("pool", lambda e, t=t: e.dma_start(out=hs[0][:, :, t * T:(t + 1) * T], in_=h[:, :, :]),
                   reads=[("h", c) for c in range(NCH)], writes=[("hs", 0, t)])
            issue_casts((250 + NT - 1) // NT)
        issue_casts(10 ** 9)

        def do_layer(l):
            hin, hout = hs[l % 2], hs[(l + 1) % 2]
            hin_i, hout_i = l % 2, (l + 1) % 2
            if l + 1 < NL:
                pending_casts.extend(cast_list(l + 1))
            casts_per_tile = (len(pending_casts) + NT - 1) // NT if l + 1 < NL else 0
            tr.op("act", lambda e, l=l: e.activation(out=esink[:, :], in_=vecs[:, l * V_LAYER + V_SINK:l * V_LAYER + V_SINK + 8], func=AF.Exp),
                  reads=["vecs"], writes=["esink"])

            tr.dma("sp", lambda e, l=l: e.dma_start(out=wple[:, :, :], in_=wb_pl[l].rearrange("k p n -> p k n")),
                   reads=[("wb_pl", l, 0), ("wb_pl", l, 1)], writes=["wple"])
            actf = act[:, :, :].rearrange("p a b -> p (a b)").bitcast(F32)

            def h1_ap(t, c):
                if t % 2 == 0:
                    return h[:, c, :]
                return acc[:, c, :] if c < 8 else actf[:, (c - 8) * T:(c - 7) * T]

            def h1_keys(t, c):
                if t % 2 == 0:
                    return [("h", c)]
                return [("acc", c), ("accs", c // 4)] if c < 8 else [("act", 2 * (c - 8)), ("act", 2 * (c - 8) + 1)]

            def h1_load(t):
                if t % 2 == 0:
                    tr.dma("pool", lambda e: e.dma_start(out=h[:, :, :], in_=hin[:, :, t * T:(t + 1) * T]),
                           reads=[("hs", hin_i, t)], writes=[("h", c) for c in range(NCH)])
                else:
                    tr.dma("pool", lambda e: e.dma_start(out=acc[:, :, :], in_=hin[:, 0:8, t * T:(t + 1) * T]),
                           reads=[("hs", hin_i, t)], writes=[k_ for c in range(8) for k_ in h1_keys(t, c)])
                    tr.dma("pool", lambda e: e.dma_start(out=actf[:, 0:8 * T].rearrange("p (a b) -> p a b", b=T), in_=hin[:, 8:16, t * T:(t + 1) * T]),
                           reads=[("hs", hin_i, t)], writes=[k_ for c in range(8, 16) for k_ in h1_keys(t, c)])

            def pass1_tile(t):
                if t == 0:
                    h1_load(0)
                tr.dma("pool", lambda e, t=t: e.dma_start(out=ropet[:, :, :], in_=rope_d[t].rearrange("a p n -> p a n")), writes=["rope"])
                if t + 1 < NT:
                    h1_load(t + 1)
                if stage < 3.02:
                    return
                norm_stats(lambda c: h1_keys(t, c), lambda c: h1_ap(t, c), xa, lambda c: ("xa", c), NCH, onesd, 0)
                if stage < 3.08:
                    return
                for c in range(NCH):
                    tr.op("dve", lambda e, c=c: e.scalar_tensor_tensor(out=xa[:, c, :], in0=h1_ap(t, c), scalar=vcol(l, V_GMIX, c), in1=rstd[:, 0, :],
                                                                        op0=ALU.mult, op1=ALU.mult),
                          reads=h1_keys(t, c) + [("rstd", 0), "vecs"], writes=[("xa", c)])
                xa_keys = [("xa", c) for c in range(NCH)]
                if stage < 3.2:
                    return
                for j in range(10):
                    wv, rk = wload(wb_in[l, j], ("wb_in", l, j))
                    bank = j % 4
                    if j == 0:
                        mm_group(ps[:, bank, :], ("ps", bank), [(wv[:, kc, :], xa[:, kc, :]) for kc in range(NCH)], reads=[rk],
                                 part_reads=[[("xa", kc)] for kc in range(NCH)])
                    else:
                        mm_group(ps[:, bank, :], ("ps", bank), [(wv[:, kc, :], xa[:, kc, :]) for kc in range(NCH)], reads=[rk] + xa_keys)
                    if j < 8:
                        dst, dkey = qb[:, j, :], ("q", j)
                    else:
                        dst, dkey = kb[:, j - 8, 0:T], ("k", j - 8)
                    tr.op("act", lambda e, dst=dst, bank=bank: e.activation(out=dst, in_=ps[:, bank, :], func=AF.Copy),
                          reads=[("ps", bank)], writes=[(dkey, "hi"), (dkey, "lo")])
                    tr.op("dve", lambda e, bank=bank: e.tensor_tensor(out=zc[:, 0, :], in0=ps[0:32, bank, :], in1=ropet[:, 0, :], op=ALU.mult),
                          reads=[("ps", bank), "rope", (dkey, "hi")], writes=[("zc", 0)])
                    mm_group(ps[0:32, 5, :], ("ps", 5), [(pswap, dst[0:32, :])], reads=[(dkey, "lo")])
                    tr.op("dve", lambda e: e.tensor_tensor(out=zc[:, 1, :], in0=ps[0:32, 5, :], in1=ropet[:, 1, :], op=ALU.mult),
                          reads=[("ps", 5), "rope"], writes=[("zc", 1)])
                    tr.op("dve", lambda e, dst=dst: e.tensor_tensor(out=dst[0:32, :], in0=zc[:, 0, :], in1=zc[:, 1, :], op=ALU.add),
                          reads=[("zc", 0), ("zc", 1)], writes=[(dkey, "lo")])
                    if j < 8:
                        tr.dma("pool", lambda e, j=j, t=t: e.dma_start(out=qs[:, j, t * T:(t + 1) * T], in_=qb[:, j, :]),
                               reads=[(dkey, "hi"), (dkey, "lo")], writes=[("qs", t, j)])
                    else:
                        tr.dma("pool", lambda e, j=j, t=t: e.dma_start(out=ks[:, j - 8, 128 + t * T:128 + (t + 1) * T], in_=kb[:, j - 8, 0:T]),
                               reads=[(dkey, "hi"), (dkey, "lo")], writes=[("ks", t, j - 8)])
                if stage < 3.3:
                    return
                wv0, rk0 = wload(wb_v[l, 0], ("wb_v", l, 0))
                wv1, rk1 = wload(wb_v[l, 1], ("wb_v", l, 1))
                for b in range(4):
                    bank = 6 + (b % 2)
                    parts = [((xa[:, kc, b * 128:(b + 1) * 128]), (wv0 if kc < 8 else wv1)[:, kc % 8, :]) for kc in range(NCH)]
                    mm_group(ps[:, bank, 0:256], ("ps", bank), parts, reads=[rk0, rk1] + xa_keys)
                    tr.op("act", lambda e, b=b, bank=bank: e.activation(out=vb[:, b, :, :].rearrange("p g d -> p (g d)"), in_=ps[:, bank, 0:256], func=AF.Copy),
                          reads=[("ps", bank)], writes=[("v", b)])
                    tr.dma("pool", lambda e, b=b, t=t: e.dma_start(out=vs[128 + t * T + b * 128:128 + t * T + (b + 1) * 128, :, :], in_=vb[:, b, :, :]),
                           reads=[("v", b)], writes=[("vs", t, b)])
                if stage < 3.4:
                    return
                for j in range(8):
                    wa, rka = wload(wb_in[l, 10 + j], ("wb_in", l, 10 + j))
                    wbb, rkb = wload(wb_in[l, 18 + j], ("wb_in", l, 18 + j))
                    ba, bb = (0, 1) if j % 2 == 0 else (2, 3)
                    mm_group(ps[:, ba, :], ("ps", ba), [(wa[:, kc, :], xa[:, kc, :]) for kc in range(NCH)], reads=[rka] + xa_keys)
                    mm_group(ps[:, bb, :], ("ps", bb), [(wbb[:, kc, :], xa[:, kc, :]) for kc in range(NCH)], reads=[rkb] + xa_keys)
                    sl = j % 2
                    tr.op("act", lambda e, bb=bb, sl=sl: e.activation(out=tmpf[:, sl, :], in_=ps[:, bb, :], func=AF.Sigmoid),
                          reads=[("ps", bb)], writes=[("tmpf", sl)])
                    tr.op("dve", lambda e, ba=ba, sl=sl, j=j: e.tensor_tensor(out=ctmp[:, j, :], in0=ps[:, ba, :], in1=tmpf[:, sl, :], op=ALU.mult),
                          reads=[("ps", ba), ("tmpf", sl)], writes=[("ctmp", j)])
                    tr.dma("pool", lambda e, j=j, t=t: e.dma_start(out=gs[:, j, 16 + t * T:16 + (t + 1) * T], in_=ctmp[:, j, :]),
                           reads=[("ctmp", j)], writes=[("gs", t, j)])

            for t_ in range(NT):
                pass1_tile(t_)

            def flags(t):
                fl = vecs[:, NL * V_LAYER + 16 + 2 * t:NL * V_LAYER + 16 + 2 * t + 1]
                fr = vecs[:, NL * V_LAYER + 16 + 2 * t + 1:NL * V_LAYER + 16 + 2 * t + 2]
                return fl, fr

            def conv_gen(t):
                fl, fr = flags(t)
                greads = [("gs", tt, j) for tt in (t - 1, t, t + 1) if 0 <= tt < NT for j in range(8)]
                if t == 0:
                    greads.append(("gs", -1))
                if t == NT - 1:
                    greads.append(("gs", NT))
                tr.dma("pool", lambda e: e.dma_start(out=gw[:, :, :], in_=gs[:, :, t * T:t * T + 544]), reads=greads, writes=["gw"])
                tr.op("dve", lambda e: e.tensor_scalar(out=gw[:, :, 0:16], in0=gw[:, :, 0:16], scalar1=fl, scalar2=None, op0=ALU.mult),
                      reads=["gw", "vecs"], writes=["gw"])
                tr.op("dve", lambda e: e.tensor_scalar(out=gw[:, :, 528:544], in0=gw[:, :, 528:544], scalar1=fr, scalar2=None, op0=ALU.mult),
                      reads=["gw", "vecs"], writes=["gw"])
                yield
                for k in range(CONVK):
                    for ch in range(8):
                        wcol = vecs[:, l * V_LAYER + V_CW + ch * CONVK + k:l * V_LAYER + V_CW + ch * CONVK + k + 1]
                        src = gw[:, ch, k + 1:k + 1 + T]
                        if k == 0:
                            tr.op("dve", lambda e, ch=ch, wcol=wcol, src=src: e.tensor_scalar(out=acc[:, ch, :], in0=src, scalar1=wcol, scalar2=vcol(l, V_CB, ch),
                                                                                         op0=ALU.mult, op1=ALU.add),
                                  reads=["gw", "vecs"], writes=[("acc", ch), ("accs", ch // 4)])
                        else:
                            tr.op("dve", lambda e, ch=ch, wcol=wcol, src=src: e.scalar_tensor_tensor(out=acc[:, ch, :], in0=src, scalar=wcol, in1=acc[:, ch, :],
                                                                                                op0=ALU.mult, op1=ALU.add),
                                  reads=["gw", "vecs", ("acc", ch)], writes=[("acc", ch)])
                        yield
                for ch in range(8):
                    tr.op("act", lambda e, ch=ch: e.activation(out=ctmp[:, ch, :], in_=acc[:, ch, :], func=AF.Copy),
                          reads=[("acc", ch)], writes=[("ctmp", ch)])
                    yield
                mm_group(ps[:, 6, :], ("ps", 6), [(onesc[:, :], ctmp[:, ch, :]) for ch in range(8)], reads=[("ctmp", ch) for ch in range(8)])
                yield
                for ch in range(8):
                    tr.op("dve", lambda e, ch=ch: e.tensor_tensor(out=acc[:, ch, :], in0=acc[:, ch, :], in1=ps[:, 6, :], op=ALU.subtract),
                          reads=[("acc", ch), ("ps", 6)], writes=[("acc", ch)])
                    yield
                norm_stats(lambda c: ("acc", c), lambda c: acc[:, c, :], ctmp, lambda c: ("ctmp", c), 8, onesc, 1, pbank=7)
                yield
                for ch in range(8):
                    tr.op("dve", lambda e, ch=ch: e.tensor_tensor(out=acc[:, ch, :], in0=acc[:, ch, :], in1=rstd[:, 1, :], op=ALU.mult),
                          reads=[("acc", ch), ("rstd", 1)], writes=[("acc", ch)])
                    tr.op("act", lambda e, ch=ch: e.activation(out=acc[:, ch, :], in_=acc[:, ch, :], func=AF.Silu,
                                                               scale=vcol(l, V_LNG, ch), bias=vcol(l, V_LNB, ch)),
                          reads=[("acc", ch), "vecs"], writes=[("acc", ch)])
                    yield
                norm_stats(lambda c: ("acc", c), lambda c: acc[:, c, :], ctmp, lambda c: ("ctmp", c), 8, onesc, 1, pbank=6)
                yield
                for ch in range(8):
                    tr.op("dve", lambda e, ch=ch: e.scalar_tensor_tensor(out=xa[:, 8 + ch, :], in0=acc[:, ch, :], scalar=vcol(l, V_GCO, ch), in1=rstd[:, 1, :],
                                                                          op0=ALU.mult, op1=ALU.mult),
                          reads=[("acc", ch), ("rstd", 1), "vecs"], writes=[("xa", 8 + ch)])
                    yield

            cstate = {"gen": None}

            def pump(n):
                g_ = cstate["gen"]
                if g_ is None:
                    return
                for _ in range(n):
                    try:
                        next(g_)
                    except StopIteration:
                        cstate["gen"] = None
                        return

            def attention(t):
                qkeys = lambda hh: [(("q", hh), "hi"), (("q", hh), "lo")]
                kkeys = lambda g: [(("k", g), "hi"), (("k", g), "lo")]
                items = [(n, g, hh) for n in range(4) for g in range(2) for hh in range(4)]
                SBK = [0, 1, 6]
                LAG = 2
                DENB = [4, 7]
                psT = ps[:, 5, :].bitcast(BF16)

                def rec_qk(i):
                    n, g, hh = items[i]
                    hd = g * 4 + hh
                    sbk = SBK[i % 3]
                    pslot = i % 4

                    def fnqk(e):
                        ins = None
                        for jj in range(3):
                            ins = e.matmul(ps[:, sbk, jj * 128:(jj + 1) * 128], lhsT=kb[:, g, (n + jj) * 128:(n + jj + 1) * 128],
                                           rhs=qb[:, hd, n * 128:(n + 1) * 128], start=True, stop=True)
                        return ins
                    tr.op("pe", fnqk, reads=qkeys(hd) + kkeys(g), writes=[("ps", sbk)])
                    tr.op("act", lambda e: e.activation(out=ptb[:, pslot, :, :].rearrange("p a b -> p (a b)"), in_=ps[:, sbk, 0:384],
                                                        func=AF.Exp, scale=SCALE),
                          reads=[("ps", sbk)], writes=[("pt", pslot)])
                    mv = 1 if n == 0 else (2 if n == 3 else 0)
                    tr.op("dve", lambda e: e.tensor_tensor(out=ptb[:, pslot, :, :].rearrange("p a b -> p (a b)"), in0=ptb[:, pslot, :, :].rearrange("p a b -> p (a b)"),
                                                           in1=m3[:, mv, :], op=ALU.mult),
                          reads=[("pt", pslot), ("m3", mv)], writes=[("pt", pslot)])

                def rec_pv(i):
                    n, g, hh = items[i]
                    hd = g * 4 + hh
                    ob = 2 + g
                    pslot = i % 4

                    def fnpv(e):
                        ins = None
                        for jj in range(3):
                            ins = e.matmul(ps[:, ob, hh * 128:(hh + 1) * 128], lhsT=ptb[:, pslot, jj, :], rhs=vb[:, n + jj, g, :],
                                           start=(jj == 0), stop=(jj == 2))
                        for jj in range(3):
                            ins = e.matmul(ps[:, DENB[g], hd:hd + 1], lhsT=ptb[:, pslot, jj, :], rhs=ones_col,
                                           start=(jj == 0), stop=(jj == 2))
                        return ins
                    wr = [("pso", ob, hh), ("psd", hd)]
                    if hh == 0:
                        wr.append(("ps", ob))
                    if i == 0 or i == 4:
                        wr.append(("ps", DENB[g]))
                    tr.op("pe", fnpv, reads=[("pt", pslot)] + [("v", n + jj) for jj in range(3)], writes=wr)

                def rec_epiA(n, g):
                    ob = 2 + g
                    hs_ = range(4 * g, 4 * g + 4)
                    tr.op("dve", lambda e: e.tensor_tensor(out=small[:, 4 * g:4 * g + 4], in0=ps[:, DENB[g], 4 * g:4 * g + 4], in1=esink[:, 4 * g:4 * g + 4], op=ALU.add),
                          reads=[("psd", hd) for hd in hs_] + ["esink", ("ps", DENB[g])], writes=[("den", g)])
                    tr.op("dve", lambda e: e.reciprocal(out=small[:, 8 + 4 * g:12 + 4 * g], in_=small[:, 4 * g:4 * g + 4]), reads=[("den", g)], writes=[("rden", g)])
                    for hd in hs_:
                        hh = hd % 4
                        tr.op("dve", lambda e, hd=hd, hh=hh: e.tensor_scalar(out=attn[:, hd, :], in0=ps[:, ob, hh * 128:(hh + 1) * 128],
                                                                           scalar1=small[:, 8 + hd:9 + hd], scalar2=None, op0=ALU.mult),
                              reads=[("pso", ob, hh), ("ps", ob), ("rden", g)], writes=[("attn", hd)])

                def rec_epiB(n):
                    tr.op("act", lambda e: e.activation(out=junk[:, :], in_=attn[:, :, :].rearrange("p a b -> p (a b)"), func=AF.Square),
                          reads=[("attn", hd) for hd in range(8)], writes=["junk"])
                    tr.op("dve", lambda e: e.reduce_sum(out=small[:, 16:17], in_=junk[:, :], axis=AX.X),
                          reads=["junk"], writes=["ass"])
                    tr.op("act", lambda e: e.activation(out=small[:, 17:18], in_=small[:, 16:17], func=AF.Sqrt, bias=epsb[:, 0:1], scale=1.0 / 1024),
                          reads=["ass"], writes=["ams"])
                    tr.op("dve", lambda e: e.reciprocal(out=small[:, 18:19], in_=small[:, 17:18]),
                          reads=["ams"], writes=["arstd"])
                    tr.op("dve", lambda e: e.tensor_scalar(out=attn_n[:, :, :], in0=attn[:, :, :], scalar1=small[:, 18:19], scalar2=None, op0=ALU.mult),
                          reads=["arstd"] + [("attn", hd) for hd in range(8)], writes=["attn_n"])

                def rec_tr(n):
                    def fntr(e):
                        ins = None
                        for c in range(8):
                            ins = e.transpose(psT[:, c * 128:(c + 1) * 128], attn_n[:, c, :], ident_b)
                        return ins
                    tr.op("pe", fntr, reads=["attn_n"], writes=[("ps", 5)])
                    for c in range(8):
                        tr.op("act", lambda e, c=c: e.activation(out=xa[:, c, n * 128:(n + 1) * 128], in_=psT[:, c * 128:(c + 1) * 128],
                                                                 func=AF.Copy, scale=vcol(l, V_GAO, c)),
                              reads=[("ps", 5), "vecs"], writes=[("xa", c)])

                pend = {}
                pendB = {}
                for i in range(LAG):
                    rec_qk(i)
                for i in range(32):
                    rec_pv(i)
                    if i + LAG < 32:
                        rec_qk(i + LAG)
                    n, g, hh = items[i]
                    if hh == 3:
                        rec_epiA(n, g)
                        if g == 1:
                            pendB[min(i + 2, 31)] = n
                    if i in pendB:
                        nb_ = pendB.pop(i)
                        rec_epiB(nb_)
                        pend[min(i + 4, 31)] = nb_
                    if i in pend:
                        rec_tr(pend.pop(i))

            def pass2_tile(t):
                issue_casts(casts_per_tile)
                fl, fr = flags(t)
                def qkv_loads(t):
                  tr.dma("pool", lambda e, t=t: e.dma_start(out=qb[:, :, :], in_=qs[:, :, t * T:(t + 1) * T]),
                         reads=[("qs", t, j) for j in range(8)], writes=[(("q", j), s_) for j in range(8) for s_ in ("hi", "lo")])
                  kreads = [("ks", tt, g) for tt in (t - 1, t, t + 1) if 0 <= tt < NT for g in range(2)]
                  if t == 0:
                      kreads.append(("ks", -1))
                  if t == NT - 1:
                      kreads.append(("ks", NT))
                  tr.dma("pool", lambda e, t=t: e.dma_start(out=kb[:, :, :], in_=ks[:, :, t * T:t * T + 768]),
                         reads=kreads, writes=[(("k", g), s_) for g in range(2) for s_ in ("hi", "lo")])
                  vreads = [("vs", tt, b) for tt in (t - 1, t, t + 1) if 0 <= tt < NT for b in range(4)]
                  if t == 0:
                      vreads.append(("vs", -1))
                  if t == NT - 1:
                      vreads.append(("vs", NT))
                  tr.dma("pool", lambda e, t=t: e.dma_start(out=vb[:, :, :, :], in_=vs[t * T:t * T + 768, :, :].rearrange("(b p) g d -> p b g d", p=128)),
                         reads=vreads, writes=[("v", b) for b in range(4)] + [("v", 4), ("v", 5)])
                if t == 0:
                    qkv_loads(0)
                tr.dma("pool", lambda e, t=t: e.dma_start(out=h[:, :, :], in_=hin[:, :, t * T:(t + 1) * T]),
                       reads=[("hs", hin_i, t)], writes=[("h", c) for c in range(NCH)])
                tr.dma("pool", lambda e, t=t: e.dma_start(out=ptok[:, :, :], in_=p_d[l, t * T:(t + 1) * T, :].rearrange("(b p) f -> p b f", p=128)),
                       writes=["ptok"])
                tr.op("dve", lambda e, fl=fl: e.tensor_scalar(out=m3[:, 1, 0:128], in0=tri_ge, scalar1=fl, scalar2=None, op0=ALU.mult),
                      reads=["vecs"], writes=[("m3", 1)])
                tr.op("dve", lambda e, fr=fr: e.tensor_scalar(out=m3[:, 2, 256:384], in0=tri_le, scalar1=fr, scalar2=None, op0=ALU.mult),
                      reads=["vecs"], writes=[("m3", 2)])
                if t == 0:
                    cstate["gen"] = conv_gen(0)
                pump(10 ** 9)

                attention(t)
                if t + 1 < NT:
                    qkv_loads(t + 1)

                xa_keys = [("xa", c) for c in range(NCH)]
                if dbg and t == 0 and l == 0:
                    tr.dma("pool", lambda e: e.dma_start(out=dbg_mixed[:, :, :], in_=xa[:, :, :]), reads=xa_keys, writes=["dbgm"])
                for j in range(NCH):
                    wv, rk = wload(wb_out[l, j], ("wb_out", l, j))
                    bank = j % 4
                    mm_group(ps[:, bank, :], ("ps", bank), [(wv[:, kc, :], xa[:, kc, :]) for kc in range(NCH)], reads=[rk] + xa_keys)
                    tr.op("dve", lambda e, j=j, bank=bank: e.tensor_tensor(out=h[:, j, :], in0=h[:, j, :], in1=ps[:, bank, :], op=ALU.add),
                          reads=[("ps", bank), ("h", j)], writes=[("h", j)])
                if dbg and t == 0 and l == 0:
                    tr.dma("pool", lambda e: e.dma_start(out=dbg_h1[:, :, :], in_=h[:, :, :]), reads=[("h", c) for c in range(NCH)], writes=["dbgh1"])
                norm_stats(lambda c: ("h", c), lambda c: h[:, c, :], xb, lambda c: ("xb", c), NCH, onesd, 0)
                for c in range(NCH):
                    tr.op("dve", lambda e, c=c: e.scalar_tensor_tensor(out=xb[:, c, :], in0=h[:, c, :], scalar=vcol(l, V_GFFN, c), in1=rstd[:, 0, :],
                                                                        op0=ALU.mult, op1=ALU.mult),
                          reads=[("h", c), ("rstd", 0), "vecs"], writes=[("xb", c)])
                if t + 1 < NT:
                    cstate["gen"] = conv_gen(t + 1)
                xb_keys = [("xb", c) for c in range(NCH)]
                for half in range(2):
                    for jj in range(NFH):
                        j = half * NFH + jj
                        wg, rkg = wload(wb_gate[l, j], ("wb_gate", l, j))
                        wu, rku = wload(wb_up[l, j], ("wb_up", l, j))
                        ba, bb = (0, 1) if jj % 2 == 0 else (2, 3)
                        if j == 0:
                            mm_group(ps[:, ba, :], ("ps", ba), [(wg[:, kc, :], xb[:, kc, :]) for kc in range(NCH)], reads=[rkg],
                                     part_reads=[[("xb", kc)] for kc in range(NCH)])
                        else:
                            mm_group(ps[:, ba, :], ("ps", ba), [(wg[:, kc, :], xb[:, kc, :]) for kc in range(NCH)], reads=[rkg] + xb_keys)
                        mm_group(ps[:, bb, :], ("ps", bb), [(wu[:, kc, :], xb[:, kc, :]) for kc in range(NCH)], reads=[rku] + xb_keys)
                        sl = jj % 2
                        tr.op("act", lambda e, ba=ba, sl=sl: e.activation(out=tmpf[:, sl, :], in_=ps[:, ba, :], func=AF.Silu),
                              reads=[("ps", ba)], writes=[("tmpf", sl)])
                        tr.op("dve", lambda e, bb=bb, sl=sl, jj=jj: e.tensor_tensor(out=act[:, jj, :], in0=ps[:, bb, :], in1=tmpf[:, sl, :], op=ALU.mult),
                              reads=[("ps", bb), ("tmpf", sl)], writes=[("act", jj)])
                        pump(6)
                    act_keys = [("act", jj) for jj in range(NFH)]
                    for j in range(NCH):
                        w0, rk0_ = wload(wb_down[l, j, half * 2], ("wb_down", l, j, half * 2))
                        w1, rk1_ = wload(wb_down[l, j, half * 2 + 1], ("wb_down", l, j, half * 2 + 1))
                        bank = j % 4
                        parts = [((w0 if kk < 11 else w1)[:, kk % 11, :], act[:, kk, :]) for kk in range(NFH)]
                        mm_group(ps[:, bank, :], ("ps", bank), parts, reads=[rk0_, rk1_] + act_keys)
                        tr.op("dve", lambda e, j=j, bank=bank: e.tensor_tensor(out=h[:, j, :], in0=h[:, j, :], in1=ps[:, bank, :], op=ALU.add),
                              reads=[("ps", bank), ("h", j)], writes=[("h", j)])
                        pump(3)
                if dbg and t == 0 and l == 0:
                    tr.dma("pool", lambda e: e.dma_start(out=dbg_h2[:, :, :], in_=h[:, :, :]), reads=[("h", c) for c in range(NCH)], writes=["dbgh2"])
                for c2 in range(2):
                    def fnpt(e, c2=c2):
                        ins = None
                        for b in range(4):
                            ins = e.transpose(ps[:, 6 + c2, b * 128:(b + 1) * 128], ptok[:, b, c2 * 128:(c2 + 1) * 128], ident_f)
                        return ins
                    tr.op("pe", fnpt, reads=["ptok"], writes=[("ps", 6 + c2)])
                    tr.op("act", lambda e, c2=c2: e.activation(out=pT[:, c2, :], in_=ps[:, 6 + c2, :], func=AF.Copy),
                          reads=[("ps", 6 + c2)], writes=[("pT", c2)])
                norm_stats(lambda c: ("h", c), lambda c: h[:, c, :], xb, lambda c: ("xb", c), NCH, onesd, 0)
                for c in range(NCH):
                    tr.op("dve", lambda e, c=c: e.scalar_tensor_tensor(out=xb[:, c, :], in0=h[:, c, :], scalar=vcol(l, V_GPLE, c), in1=rstd[:, 0, :],
                                                                        op0=ALU.mult, op1=ALU.mult),
                          reads=[("h", c), ("rstd", 0), "vecs"], writes=[("xb", c)])
                wp0, wp1, rkp0, rkp1 = wple[:, 0, :], wple[:, 1, :], "wple", "wple"
                for j in range(NCH):
                    wv, rk = wload(wb_pg[l, j], ("wb_pg", l, j))
                    ba, bb = (0, 1) if j % 2 == 0 else (2, 3)
                    if j == 0:
                        mm_group(ps[:, ba, :], ("ps", ba), [(wv[:, kc, :], xb[:, kc, :]) for kc in range(NCH)], reads=[rk],
                                 part_reads=[[("xb", kc)] for kc in range(NCH)])
                    else:
                        mm_group(ps[:, ba, :], ("ps", ba), [(wv[:, kc, :], xb[:, kc, :]) for kc in range(NCH)], reads=[rk] + xb_keys)
                    mm_group(ps[:, bb, :], ("ps", bb), [(wp0[:, j * 128:(j + 1) * 128], pT[:, 0, :]), (wp1[:, j * 128:(j + 1) * 128], pT[:, 1, :])],
                             reads=[rkp0, rkp1, ("pT", 0), ("pT", 1)])
                    sl = j % 2
                    tr.op("act", lambda e, ba=ba, sl=sl: e.activation(out=tmpf[:, sl, :], in_=ps[:, ba, :], func=AF.Sigmoid),
                          reads=[("ps", ba)], writes=[("tmpf", sl)])
                    tr.op("dve", lambda e, bb=bb, sl=sl: e.tensor_tensor(out=tmpf[:, sl, :], in0=ps[:, bb, :], in1=tmpf[:, sl, :], op=ALU.mult),
                          reads=[("ps", bb), ("tmpf", sl)], writes=[("tmpf", sl)])
                    tr.op("dve", lambda e, j=j, sl=sl: e.tensor_tensor(out=h[:, j, :], in0=h[:, j, :], in1=tmpf[:, sl, :], op=ALU.add),
                          reads=[("tmpf", sl), ("h", j)], writes=[("h", j)])
                    pump(3)
                pump(10 ** 9)
                if l + 1 < NL:
                    tr.dma("pool", lambda e, t=t: e.dma_start(out=hout[:, :, t * T:(t + 1) * T], in_=h[:, :, :]),
                           reads=[("h", c) for c in range(NCH)], writes=[("hs", hout_i, t)])
                else:
                    norm_stats(lambda c: ("h", c), lambda c: h[:, c, :], xb, lambda c: ("xb", c), NCH, onesd, 0)
                    gf0 = NL * V_LAYER
                    for c in range(NCH):
                        tr.op("dve", lambda e, c=c: e.scalar_tensor_tensor(out=h[:, c, :], in0=h[:, c, :], scalar=vecs[:, gf0 + c:gf0 + c + 1], in1=rstd[:, 0, :],
                                                                            op0=ALU.mult, op1=ALU.mult),
                              reads=[("h", c), ("rstd", 0), "vecs"], writes=[("h", c)])
                    for b in range(4):
                        half = b % 2
                        stg = xb[:, half * 8:(half + 1) * 8, :].rearrange("p a b -> p (a b)").bitcast(F32)
                        skey = ("xbs", half)
                        xbk = [("xb", half * 8 + i) for i in range(8)]
                        for c4 in range(4):
                            bank = 6 + (c4 % 2)

                            def fnty(e, b=b, c4=c4, bank=bank):
                                ins = None
                                for cc in range(4):
                                    c = c4 * 4 + cc
                                    ins = e.transpose(ps[:, bank, cc * 128:(cc + 1) * 128], h[:, c, b * 128:(b + 1) * 128], ident_f)
                                return ins
                            tr.op("pe", fnty, reads=[("h", c4 * 4 + cc) for cc in range(4)], writes=[("ps", bank)])
                            if c4 % 2 == 0:
                                tr.op("act", lambda e, c4=c4, bank=bank, stg=stg: e.activation(out=stg[:, c4 * 512:(c4 + 1) * 512], in_=ps[:, bank, :], func=AF.Copy),
                                      reads=[("ps", bank)], writes=([skey] + xbk) if c4 == 0 else [(skey, c4)])
                            else:
                                tr.op("dve", lambda e, c4=c4, bank=bank, stg=stg: e.tensor_copy(out=stg[:, c4 * 512:(c4 + 1) * 512], in_=ps[:, bank, :]),
                                      reads=[("ps", bank)], writes=[(skey, c4)])
                        tok0 = t * T + b * 128
                        tr.dma("pool", lambda e, stg=stg, tok0=tok0: e.dma_start(out=y_d[tok0:tok0 + 128, :], in_=stg),
                               reads=[skey] + [(skey, c4) for c4 in range(1, 4)], writes=[("y", t, b), skey] + xbk)

            for t_ in range(NT if stage >= 4 else 0):
                pass2_tile(t_)

        for l_ in range(NL if stage >= 3 else 0):
            do_layer(l_)

        tr.sync_all()
        tr.emit(block)
    return nc


def _consts():
    c = np.zeros((128, 5 * 128), np.float32)
    c[:, 0:128] = np.eye(128, dtype=np.float32)
    kk = np.arange(128)[:, None]
    tt = np.arange(128)[None, :]
    c[:, 128:256] = (kk >= tt).astype(np.float32)
    c[:, 256:384] = (kk <= tt).astype(np.float32)
    for m in range(32):
        c[(m + 16) % 32, 384 + m] = 1.0
    c[:, 512:640] = 1.0
    return c


def _rope_tables(pos):
    half = 16
    inv_freq = np.exp(np.float32(-math.log(ROPE_THETA)) * np.arange(0, 32, 2, dtype=np.float32) / np.float32(32)).astype(np.float32)
    ang = (pos.astype(np.float32)[:, None] * inv_freq[None, :]).astype(np.float32)
    cos = np.cos(ang).astype(np.float32).T
    sin = np.sin(ang).astype(np.float32).T
    nt = pos.shape[0] // T
    out = np.zeros((nt, 2, 32, T), np.float32)
    for t in range(nt):
        sl = slice(t * T, (t + 1) * T)
        out[t, 0, 0:16] = cos[:, sl]
        out[t, 0, 16:32] = cos[:, sl]
        out[t, 1, 0:16] = -sin[:, sl]
        out[t, 1, 16:32] = sin[:, sl]
    return out


def _pack_vecs(inp, NL, NT, flags):
    NV = NL * V_LAYER + 16 + 2 * NT
    v = np.zeros((128, NV), np.float32)

    def fm(a, n):
        return np.asarray(a, np.float32).reshape(n, 128).T
    for l in range(NL):
        o = l * V_LAYER
        v[:, o + V_GMIX:o + V_GMIX + 16] = fm(inp["g_mix"][l], 16)
        v[:, o + V_GFFN:o + V_GFFN + 16] = fm(inp["g_ffn"][l], 16)
        v[:, o + V_GPLE:o + V_GPLE + 16] = fm(inp["g_ple"][l], 16)
        v[:, o + V_CB:o + V_CB + 8] = fm(inp["conv_b"][l], 8)
        v[:, o + V_LNG:o + V_LNG + 8] = fm(inp["conv_ln_g"][l], 8)
        v[:, o + V_LNB:o + V_LNB + 8] = fm(inp["conv_ln_b"][l], 8)
        v[:, o + V_GCO:o + V_GCO + 8] = fm(inp["g_conv_out"][l], 8)
        v[:, o + V_GAO:o + V_GAO + 8] = fm(inp["g_attn_out"][l], 8)
        cw = np.asarray(inp["conv_w"][l], np.float32)
        cwt = cw.T.reshape(8, 128, CONVK).transpose(1, 0, 2).reshape(128, 8 * CONVK)
        v[:, o + V_CW:o + V_CW + 8 * CONVK] = cwt
        v[:, o + V_SINK:o + V_SINK + 8] = np.asarray(inp["sink"][l], np.float32)[None, :]
    o = NL * V_LAYER
    v[:, o:o + 16] = fm(inp["g_final"], 16)
    v[:, o + 16:o + 16 + 2 * NT] = flags.reshape(1, 2 * NT)
    return v


_PROG_CACHE = {}


def _run(core_inputs, NT, NL):
    key = (NT, NL)
    if key not in _PROG_CACHE:
        _PROG_CACHE[key] = build_program(NT, NL)
    nc = _PROG_CACHE[key]
    res = run_bass_kernel_spmd(nc, core_inputs, core_ids=list(range(len(core_inputs))))
    return [r["y"] for r in res.results]


def kernel(**inputs):
    inp = {k: np.asarray(v) for k, v in inputs.items()}
    NL, NT = 4, NT_FULL
    xp, xsamp = inp["x_prompt"], inp["x_sample"]
    pp, psamp = inp["p_prompt"], inp["p_sample"]
    consts = _consts()
    wkeys = ["w_in", "w_out", "w_gate", "w_up", "w_down", "w_ple_gate", "w_ple"]
    shared = {k: np.ascontiguousarray(inp[k], dtype=np.float32) for k in wkeys}
    NTOK = NT * T
    core_inputs = []
    for c in range(N_CORES):
        flags = np.ones((NT, 2), np.float32)
        flags[0, 0] = 0.0
        flags[NT - 1, 1] = 0.0
        if c < 4:
            b, half = c // 2, c % 2
            s0 = 0 if half == 0 else 16384 - NTOK
            xc = xp[b, s0:s0 + NTOK]
            pc = pp[:, b, s0:s0 + NTOK]
            pos = np.arange(s0, s0 + NTOK, dtype=np.float32)
        else:
            s = 2 * (c - 4)
            xc = np.zeros((NTOK, D), np.float32)
            pc = np.zeros((NL, NTOK, PLE), np.float32)
            xc[0:4096] = xsamp[s]
            xc[4096:8192] = xsamp[s + 1]
            pc[:, 0:4096] = psamp[:, s]
            pc[:, 4096:8192] = psamp[:, s + 1]
            pos = np.concatenate([np.arange(4096), np.arange(4096), np.arange(NTOK - 8192)]).astype(np.float32)
            flags[7, 1] = 0.0
            flags[8, 0] = 0.0
            flags[15, 1] = 0.0
            flags[16, 0] = 0.0
        m = dict(shared)
        m["x"] = np.ascontiguousarray(xc, dtype=np.float32)
        m["p"] = np.ascontiguousarray(pc, dtype=np.float32)
        m["vecs"] = _pack_vecs(inp, NL, NT, flags)
        m["rope"] = _rope_tables(pos)
        m["consts"] = consts
        core_inputs.append(m)
    ys = _run(core_inputs, NT, NL)
    y_prompt = np.empty((2, 16384, D), np.float32)
    y_sample = np.empty((8, 4096, D), np.float32)
    for c in range(N_CORES):
        yc = ys[c]
        if c < 4:
            b, half = c // 2, c % 2
            if half == 0:
                y_prompt[b, 0:8192] = yc[0:8192]
            else:
                y_prompt[b, 8192:16384] = yc[NTOK - 8192:NTOK]
        else:
            s = 2 * (c - 4)
            y_sample[s] = yc[0:4096]
            y_sample[s + 1] = yc[4096:8192]
    return (y_prompt, y_sample)
```
